# Optimizing a Trainium2 kernel written in Bass

```python
import jax
import jax.numpy as jnp
from jax import lax
import numpy as np

D_MODEL = 2048
BATCH = 2
SEQ = 4096
DEPTH = 4

EPS = 1e-6
BRANCH_WIDTH = 1024
N_BRANCH = 3
A_HEADS = 8
A_KEY = 128
A_VAL = BRANCH_WIDTH // A_HEADS
A_FDIM = A_HEADS * A_KEY
A_CHUNK = 64
B_HEADS = 8
B_HDIM = BRANCH_WIDTH // B_HEADS
B_CONV = 4
B_C = 8.0
C_HDIM = 64
C_HEADS = BRANCH_WIDTH // C_HDIM
C_GROUPS = 2
C_STATE = 128
C_CONV = 4
C_CHUNK = 128
C_CONV_DIM = BRANCH_WIDTH + 2 * C_GROUPS * C_STATE
SPLITS = (A_FDIM, A_FDIM, BRANCH_WIDTH, BRANCH_WIDTH, BRANCH_WIDTH, BRANCH_WIDTH, BRANCH_WIDTH, C_CONV_DIM, C_HEADS, N_BRANCH * D_MODEL)
IN_COLS = sum(SPLITS)

kernel_name = 'hybrid_hgrn2_rglru_ssd_gated_merge'


def rmsnorm(x, w):
    xf = x.astype(jnp.float32)
    y = xf * lax.rsqrt(jnp.mean(xf * xf, axis=-1, keepdims=True) + EPS)
    return (y * w.astype(jnp.float32)).astype(x.dtype)


def causal_depthwise_conv(x, w, b):
    width, ch = w.shape
    y = lax.conv_general_dilated(x, w[:, None, :].astype(x.dtype), window_strides=(1,),
                                 padding=[(width - 1, 0)], dimension_numbers=('NWC', 'WIO', 'NWC'),
                                 feature_group_count=ch)
    return y + b.astype(x.dtype)


def hgrn2_mixer(q, f_pre, i, lb):
    bsz, s, _ = q.shape
    n = s // A_CHUNK
    lbf = lb.astype(jnp.float32)
    log_f = jnp.logaddexp(jnp.log(lbf), jnp.log1p(-lbf) + jax.nn.log_sigmoid(f_pre.astype(jnp.float32)))
    k = -jnp.expm1(log_f)

    def to_chunks(t, d):
        return t.astype(jnp.float32).reshape(bsz, n, A_CHUNK, A_HEADS, d).transpose(1, 0, 3, 2, 4)

    qc, kc, gc, vc = to_chunks(q, A_KEY), to_chunks(k, A_KEY), to_chunks(log_f, A_KEY), to_chunks(i, A_VAL)
    causal = jnp.tril(jnp.ones((A_CHUNK, A_CHUNK), dtype=bool))

    def step(state, inp):
        qb, kb, gb, vb = inp
        b = jnp.cumsum(gb, axis=-2)
        diff = b[:, :, :, None, :] - b[:, :, None, :, :]
        decay = jnp.exp(jnp.where(causal[:, :, None], diff, -jnp.inf))
        scores = jnp.einsum('bhtk,bhsk,bhtsk->bhts', qb, kb, decay)
        o = (jnp.einsum('bhts,bhsv->bhtv', scores, vb)
             + jnp.einsum('bhtk,bhkv->bhtv', qb * jnp.exp(b), state))
        b_last = b[:, :, -1:, :]
        state = (jnp.exp(b_last[:, :, 0, :, None]) * state
                 + jnp.einsum('bhsk,bhsv->bhkv', kb * jnp.exp(b_last - b), vb))
        return state, o

    s0 = jnp.zeros((bsz, A_HEADS, A_KEY, A_VAL), jnp.float32)
    _, o = lax.scan(step, s0, (qc, kc, gc, vc))
    return o.transpose(1, 0, 3, 2, 4).reshape(bsz, s, A_HEADS, A_VAL)


def rglru_mixer(x, conv_w, conv_b, wa, ba, wx, bx, lam):
    xc = causal_depthwise_conv(x, conv_w, conv_b)
    bsz, s, _ = xc.shape
    xh = xc.astype(jnp.float32).reshape(bsz, s, B_HEADS, B_HDIM)
    r = jax.nn.sigmoid(jnp.einsum('bshi,hij->bshj', xh, wa.astype(jnp.float32)) + ba).reshape(bsz, s, BRANCH_WIDTH)
    ig = jax.nn.sigmoid(jnp.einsum('bshi,hij->bshj', xh, wx.astype(jnp.float32)) + bx).reshape(bsz, s, BRANCH_WIDTH)
    log_a = -B_C * r * jax.nn.softplus(-lam.astype(jnp.float32))
    a = jnp.exp(log_a)
    u = jnp.sqrt(-jnp.expm1(2.0 * log_a)) * (ig * xh.reshape(bsz, s, BRANCH_WIDTH))

    def combine(left, right):
        a1, b1 = left
        a2, b2 = right
        return a1 * a2, a2 * b1 + b2

    _, h = lax.associative_scan(combine, (a, u), axis=1)
    return h


def segsum(x):
    t = x.shape[-1]
    cs = jnp.cumsum(x, axis=-1)
    mask = jnp.tril(jnp.ones((t, t), dtype=bool))
    return jnp.where(mask, cs[..., :, None] - cs[..., None, :], -jnp.inf)


def ssd_mixer(xbc_pre, dt_pre, conv_w, conv_b, dt_bias, a_log, d_skip):
    xbc = jax.nn.silu(causal_depthwise_conv(xbc_pre, conv_w, conv_b)).astype(jnp.float32)
    bsz, s, _ = xbc.shape
    n = s // C_CHUNK
    rep = C_HEADS // C_GROUPS
    xs, bm, cm = jnp.split(xbc, [BRANCH_WIDTH, BRANCH_WIDTH + C_GROUPS * C_STATE], axis=-1)
    xs = xs.reshape(bsz, n, C_CHUNK, C_GROUPS, rep, C_HDIM)
    bm = bm.reshape(bsz, n, C_CHUNK, C_GROUPS, C_STATE)
    cm = cm.reshape(bsz, n, C_CHUNK, C_GROUPS, C_STATE)
    dt = jax.nn.softplus(dt_pre.astype(jnp.float32) + dt_bias.astype(jnp.float32))
    a = -jnp.exp(a_log.astype(jnp.float32))
    dtc = dt.reshape(bsz, n, C_CHUNK, C_GROUPS, rep)
    da = (dtc * a.reshape(C_GROUPS, rep)).transpose(0, 3, 4, 1, 2)
    xdt = xs * dtc[..., None]
    cum = jnp.cumsum(da, axis=-1)
    lmat = jnp.exp(segsum(da))
    cb = jnp.einsum('bclgn,bcsgn->bgcls', cm, bm)
    y_diag = jnp.einsum('bgcls,bgrcls,bcsgrp->bclgrp', cb, lmat, xdt)
    decay_states = jnp.exp(cum[..., -1:] - cum)
    states = jnp.einsum('bclgn,bgrcl,bclgrp->bcgrpn', bm, decay_states, xdt)
    chunk_decay = jnp.exp(segsum(jnp.pad(cum[..., -1], [(0, 0), (0, 0), (0, 0), (1, 0)])))
    states = jnp.concatenate([jnp.zeros_like(states[:, :1]), states], axis=1)
    states = jnp.einsum('bgrzc,bcgrpn->bzgrpn', chunk_decay, states)[:, :-1]
    y_off = jnp.einsum('bclgn,bcgrpn,bgrcl->bclgrp', cm, states, jnp.exp(cum))
    y = y_diag + y_off + d_skip.astype(jnp.float32).reshape(C_GROUPS, rep)[:, :, None] * xs
    return y.reshape(bsz, s, BRANCH_WIDTH)


def setup_inputs(seed: int = 0) -> dict:
    key = jax.random.key(seed)
    ks = jax.random.split(key, 24)
    f32 = jnp.float32

    def nrm(k, shape, scale):
        return scale * jax.random.normal(k, shape, f32)

    x = jax.random.normal(ks[0], (BATCH, SEQ, D_MODEL), f32)
    norm_w = 1.0 + nrm(ks[1], (DEPTH, D_MODEL), 0.02)
    w_in = nrm(ks[2], (DEPTH, D_MODEL, IN_COLS), D_MODEL ** -0.5)
    hgrn_lb_logits = nrm(ks[3], (DEPTH, A_FDIM), 0.5)
    hgrn_norm_w = 1.0 + nrm(ks[4], (DEPTH, BRANCH_WIDTH), 0.02)
    rglru_conv_w = nrm(ks[5], (DEPTH, B_CONV, BRANCH_WIDTH), B_CONV ** -0.5)
    rglru_conv_b = nrm(ks[6], (DEPTH, BRANCH_WIDTH), 0.02)
    rglru_wa = nrm(ks[7], (DEPTH, B_HEADS, B_HDIM, B_HDIM), B_HDIM ** -0.5)
    rglru_ba = nrm(ks[8], (DEPTH, B_HEADS, B_HDIM), 0.02)
    rglru_wx = nrm(ks[9], (DEPTH, B_HEADS, B_HDIM, B_HDIM), B_HDIM ** -0.5)
    rglru_bx = nrm(ks[10], (DEPTH, B_HEADS, B_HDIM), 0.02)
    u = jax.random.uniform(ks[11], (DEPTH, BRANCH_WIDTH), f32, 0.9, 0.999)
    p = u ** (1.0 / B_C)
    rglru_lambda = jnp.log(p) - jnp.log1p(-p)
    ssd_conv_w = nrm(ks[12], (DEPTH, C_CONV, C_CONV_DIM), C_CONV ** -0.5)
    ssd_conv_b = nrm(ks[13], (DEPTH, C_CONV_DIM), 0.02)
    dt0 = jnp.exp(jax.random.uniform(ks[14], (DEPTH, C_HEADS), f32, float(np.log(1e-3)), float(np.log(1e-1))))
    ssd_dt_bias = dt0 + jnp.log(-jnp.expm1(-dt0))
    ssd_a_log = jnp.log(jax.random.uniform(ks[15], (DEPTH, C_HEADS), f32, 1.0, 16.0))
    ssd_d = 1.0 + nrm(ks[16], (DEPTH, C_HEADS), 0.1)
    ssd_norm_w = 1.0 + nrm(ks[17], (DEPTH, BRANCH_WIDTH), 0.02)
    w_out = nrm(ks[18], (DEPTH, N_BRANCH * BRANCH_WIDTH, D_MODEL), (N_BRANCH * BRANCH_WIDTH) ** -0.5)
    final_norm_w = 1.0 + nrm(ks[19], (D_MODEL,), 0.02)
    return {'x': x, 'norm_w': norm_w, 'w_in': w_in, 'hgrn_lb_logits': hgrn_lb_logits,
            'hgrn_norm_w': hgrn_norm_w, 'rglru_conv_w': rglru_conv_w, 'rglru_conv_b': rglru_conv_b,
            'rglru_wa': rglru_wa, 'rglru_ba': rglru_ba, 'rglru_wx': rglru_wx, 'rglru_bx': rglru_bx,
            'rglru_lambda': rglru_lambda, 'ssd_conv_w': ssd_conv_w, 'ssd_conv_b': ssd_conv_b,
            'ssd_dt_bias': ssd_dt_bias, 'ssd_a_log': ssd_a_log, 'ssd_d': ssd_d, 'ssd_norm_w': ssd_norm_w,
            'w_out': w_out, 'final_norm_w': final_norm_w}


def reference(x, norm_w, w_in, hgrn_lb_logits, hgrn_norm_w, rglru_conv_w, rglru_conv_b, rglru_wa, rglru_ba,
              rglru_wx, rglru_bx, rglru_lambda, ssd_conv_w, ssd_conv_b, ssd_dt_bias, ssd_a_log, ssd_d,
              ssd_norm_w, w_out, final_norm_w):
    split_idx = np.cumsum(SPLITS)[:-1].tolist()
    sm = jax.nn.softmax(hgrn_lb_logits.astype(jnp.float32), axis=0)
    cs = jnp.cumsum(sm, axis=0)
    lower_bounds = cs - cs[0:1]
    bsz, s, _ = x.shape
    for l in range(DEPTH):
        h = rmsnorm(x, norm_w[l])
        proj = h @ w_in[l]
        a_q, a_f, a_i, a_z, b_x, b_z, c_z, c_xbc, c_dt, gate_logits = jnp.split(proj, split_idx, axis=-1)
        o_a = hgrn2_mixer(a_q, a_f, a_i, lower_bounds[l])
        o_a = rmsnorm(o_a, hgrn_norm_w[l].reshape(A_HEADS, A_VAL)).reshape(bsz, s, BRANCH_WIDTH)
        o_a = o_a * jax.nn.silu(a_z)
        o_b = rglru_mixer(b_x, rglru_conv_w[l], rglru_conv_b[l], rglru_wa[l], rglru_ba[l], rglru_wx[l],
                          rglru_bx[l], rglru_lambda[l]) * jax.nn.silu(b_z)
        y_c = ssd_mixer(c_xbc, c_dt, ssd_conv_w[l], ssd_conv_b[l], ssd_dt_bias[l], ssd_a_log[l], ssd_d[l])
        o_c = rmsnorm((y_c * jax.nn.silu(c_z)).reshape(bsz, s, C_GROUPS, BRANCH_WIDTH // C_GROUPS),
                      ssd_norm_w[l].reshape(C_GROUPS, BRANCH_WIDTH // C_GROUPS)).reshape(bsz, s, BRANCH_WIDTH)
        branches = jnp.stack([o_a, o_b, o_c], axis=2).astype(x.dtype)
        branch_out = jnp.einsum('bsnc,ncd->bsnd', branches,
                                w_out[l].reshape(N_BRANCH, BRANCH_WIDTH, D_MODEL))
        gates = jax.nn.sigmoid(gate_logits.astype(jnp.float32).reshape(bsz, s, N_BRANCH, D_MODEL))
        x = x + jnp.sum(gates * branch_out, axis=2).astype(x.dtype)
    return rmsnorm(x, final_norm_w)
```

```python
import numpy as np
import concourse.bass as bass
import concourse.mybir as mybir

F32 = mybir.dt.float32
BF16 = mybir.dt.bfloat16
AF = mybir.ActivationFunctionType
ALU = mybir.AluOpType


def is_ap(x):
    return hasattr(x, "ap") and hasattr(x, "tensor") and hasattr(x, "offset")


class Sched:
    ENGS = ["pe", "act", "dve", "pool", "sp"]

    def __init__(self, nc, stack):
        self.nc = nc
        self.stack = stack
        self.ops = {e: [] for e in self.ENGS}
        self.sem = {}
        self.cnt = {}
        self.seen = {e: {} for e in self.ENGS}
        self.state = {}
        self.tinfo = {}
        for e in self.ENGS:
            self.newsem("E:" + e)
        self.nops = 0

    def newsem(self, key):
        if key not in self.sem:
            self.sem[key] = self.stack.enter_context(self.nc.semaphore("s%d" % len(self.sem)))
            self.cnt[key] = 0
        return self.sem[key]

    def sb(self, name, shape, dtype, blk=None):
        t = self.stack.enter_context(self.nc.sbuf_tensor(name, list(shape), dtype))
        psize = int(np.prod(shape[1:]))
        self.tinfo[name] = (psize, blk or psize, mybir.dt.size(dtype))
        return t

    def ps(self, name, shape, dtype, blk=None):
        t = self.stack.enter_context(self.nc.psum_tensor(name, list(shape), dtype))
        psize = int(np.prod(shape[1:]))
        self.tinfo[name] = (psize, blk or psize, mybir.dt.size(dtype))
        return t

    def dram(self, name, shape, dtype, kind=None, blk=None, track=True):
        if kind is None:
            t = self.nc.dram_tensor(name, list(shape), dtype)
        else:
            t = self.nc.dram_tensor(name, list(shape), dtype, kind=kind)
        if track:
            self.tinfo[name] = (None, blk or int(np.prod(shape)), mybir.dt.size(dtype))
        return t.ap()

    def _blocks(self, ap):
        name = ap.name
        info = self.tinfo.get(name)
        if info is None:
            return []
        psize, blk, esz = info
        off = int(ap.offset)
        pairs = ap.ap
        vsz = mybir.dt.size(ap.dtype)
        if psize is not None:
            vps = psize * esz // vsz
            lo = off % vps
            hi = lo + sum((c - 1) * abs(s) for s, c in pairs[1:])
        else:
            lo = off
            hi = lo + sum((c - 1) * abs(s) for s, c in pairs)
        lo = lo * vsz // esz
        hi = hi * vsz // esz
        return [(name, b) for b in range(lo // blk, hi // blk + 1)]

    def _deps(self, eng, outs, ins):
        waits = {}
        me = "E:" + eng
        seen = self.seen[eng]

        def need(ev):
            if ev is None:
                return
            sid, val = ev
            if sid == me and eng == "pe":
                return
            if seen.get(sid, 0) >= val:
                return
            if waits.get(sid, 0) < val:
                waits[sid] = val

        for ap in ins:
            for b in self._blocks(ap):
                st = self.state.get(b)
                if st:
                    need(st[0])
        for ap in outs:
            for b in self._blocks(ap):
                st = self.state.get(b)
                if st:
                    need(st[0])
                    for sid, val in st[1].items():
                        need((sid, val))
        for sid, val in waits.items():
            seen[sid] = val
        return list(waits.items())

    def _record(self, ev, outs, ins):
        sid, val = ev
        for ap in ins:
            for b in self._blocks(ap):
                st = self.state.setdefault(b, [None, {}])
                if st[1].get(sid, 0) < val:
                    st[1][sid] = val
        for ap in outs:
            for b in self._blocks(ap):
                self.state[b] = [ev, {}]

    def add(self, eng, fn, outs, ins):
        outs = [a for a in outs if is_ap(a)]
        ins = [a for a in ins if is_ap(a)]
        waits = self._deps(eng, outs, ins)
        sid = "E:" + eng
        self.cnt[sid] += 1
        ev = (sid, self.cnt[sid])
        self._record(ev, outs, ins)
        self.ops[eng].append((waits, fn, sid, 1))
        self.nops += 1

    def dma(self, queue, semkey, out, in_, **kw):
        self.newsem(semkey)
        waits = self._deps(queue, [out], [in_])
        self.cnt[semkey] += 16
        ev = (semkey, self.cnt[semkey])
        self._record(ev, [out], [in_])
        self.ops[queue].append((waits, lambda e: e.dma_start(out=out, in_=in_, **kw), semkey, 16))
        self.nops += 1

    def coll(self, semkey, kind, groups, in_ap, out_ap):
        self.newsem(semkey)
        assert self.cnt[semkey] == 0
        waits = self._deps("pool", [out_ap], [in_ap])
        self.cnt[semkey] = 1
        ev = (semkey, 1)
        self._record(ev, [out_ap], [in_ap])
        self.ops["pool"].append((waits, lambda e: e.collective_compute(
            kind, ALU.bypass, replica_groups=groups, ins=[in_ap.opt()], outs=[out_ap.opt()]), semkey, None))
        self.nops += 1

    def wait_all(self, eng):
        waits = []
        for sid, val in self.cnt.items():
            if sid == "E:" + eng or val == 0:
                continue
            if self.seen[eng].get(sid, 0) < val:
                waits.append((sid, val))
                self.seen[eng][sid] = val
        self.ops[eng].append((waits, None, None, 0))

    def emit(self):
        nc = self.nc
        with nc.Block() as block:
            def mk(engname):
                def body(e):
                    for waits, fn, sid, inc in self.ops[engname]:
                        for wsid, wval in waits:
                            e.wait_ge(self.sem[wsid], wval)
                        if fn is None:
                            continue
                        ins = fn(e)
                        if inc is None:
                            ins.then_inc(self.sem[sid])
                        else:
                            ins.then_inc(self.sem[sid], inc)
                return body
            block.tensor(mk("pe"))
            block.scalar(mk("act"))
            block.vector(mk("dve"))
            block.gpsimd(mk("pool"))
            block.sync(mk("sp"))

    def act(self, out, in_, func, bias=None, scale=None, accum_out=None):
        kw = {}
        if bias is not None:
            kw["bias"] = bias
        if scale is not None:
            kw["scale"] = scale
        if accum_out is not None:
            kw["accum_out"] = accum_out
        self.add("act", lambda e: e.activation(out=out, in_=in_, func=func, **kw),
                 [out, accum_out], [in_, bias, scale])

    def tt(self, out, in0, in1, op, eng="dve"):
        self.add(eng, lambda e: e.tensor_tensor(out=out, in0=in0, in1=in1, op=op), [out], [in0, in1])

    def ts(self, out, in0, s1, s2=None, op0=ALU.mult, op1=None, eng="dve", accum_out=None):
        kw = {}
        if op1 is not None:
            kw["op1"] = op1
        if accum_out is not None:
            kw["accum_out"] = accum_out
        self.add(eng, lambda e: e.tensor_scalar(out=out, in0=in0, scalar1=s1, scalar2=s2, op0=op0, **kw),
                 [out, accum_out], [in0, s1, s2])

    def stt(self, out, in0, scalar, in1, op0, op1, eng="dve"):
        self.add(eng, lambda e: e.scalar_tensor_tensor(out=out, in0=in0, scalar=scalar, in1=in1, op0=op0, op1=op1),
                 [out], [in0, scalar, in1])

    def copy(self, out, in_, eng="dve"):
        if eng == "act":
            self.add("act", lambda e: e.copy(out=out, in_=in_), [out], [in_])
        else:
            self.add(eng, lambda e: e.tensor_copy(out=out, in_=in_), [out], [in_])

    def memset(self, ap, val, eng="dve"):
        self.add(eng, lambda e: e.memset(ap, val), [ap], [])

    def recip(self, out, in_):
        self.add("dve", lambda e: e.reciprocal(out=out, in_=in_), [out], [in_])

    def scan(self, out, d0, d1, initial, op0=ALU.mult, op1=ALU.add):
        self.add("dve", lambda e: e.tensor_tensor_scan(out=out, data0=d0, data1=d1, initial=initial, op0=op0, op1=op1),
                 [out], [d0, d1, initial])

    def mm(self, out, pairs, start=True, stop=True):
        pairs = list(pairs)
        n = len(pairs)

        def fn(e):
            ins = None
            for i, (l, r) in enumerate(pairs):
                ins = e.matmul(out, l, r, start=(start and i == 0), stop=(stop and i == n - 1))
            return ins
        self.add("pe", fn, [out], [a for p in pairs for a in p])

    def transpose(self, out, in_, ident):
        self.add("pe", lambda e: e.transpose(out, in_, ident), [out], [in_, ident])


from contextlib import ExitStack
from concourse.bass_utils import run_bass_kernel_spmd

U8 = mybir.dt.uint8
T = 1024
KC = 16
G = 256
NG_MAIN = 34
NG_GATE = 24
COL_AQ, COL_AF, COL_AI, COL_AZ, COL_BX, COL_BZ, COL_CZ, COL_CX, COL_DT, COL_G = (
    0, 1024, 2048, 3072, 4096, 5120, 6144, 7168, 8704, 8720)
EPS = 1e-6

PP_LBL, PP_FNW, PP_L0 = 0, 32, 48
PL = 156
O_NW, O_HNW, O_RCW, O_RCB, O_BA, O_BX, O_LAM, O_SCW, O_SCB, O_SNW = 0, 16, 24, 56, 64, 72, 80, 88, 136, 148
NP = PP_L0 + 4 * PL
RB_L = 48
NR = 4 * RB_L
C_ID, C_TRI, C_MA, C_NEG, NCST = 0, 128, 256, 384, 512


def fm(v, n):
    return np.ascontiguousarray(v.reshape(n, 128).T)


def host_consts():
    c = np.zeros((128, NCST), np.float32)
    i = np.arange(128)
    c[:, C_ID:C_ID + 128] = np.eye(128, dtype=np.float32)
    c[:, C_TRI:C_TRI + 128] = (i[:, None] <= i[None, :]).astype(np.float32)
    c[:, C_MA:C_MA + 128] = ((i[:, None] <= i[None, :]) & (i[:, None] // 64 == i[None, :] // 64)).astype(np.float32)
    c[:, C_NEG:C_NEG + 128] = np.where(i[None, :] < i[:, None], -30000.0, 0.0).astype(np.float32)
    return c


def host_prep(inp, L=4):
    f32 = np.float32
    pp = np.zeros((128, NP), f32)
    lbl = inp["hgrn_lb_logits"].astype(f32)
    pp[:, PP_LBL:PP_LBL + 32] = lbl.reshape(4, 8, 128).transpose(2, 0, 1).reshape(128, 32)
    pp[:, PP_FNW:PP_FNW + 16] = fm(inp["final_norm_w"].astype(f32), 16)
    rb = np.zeros((128, NR), f32)
    for l in range(4):
        b = PP_L0 + l * PL
        pp[:, b + O_NW:b + O_NW + 16] = fm(inp["norm_w"][l], 16)
        pp[:, b + O_HNW:b + O_HNW + 8] = fm(inp["hgrn_norm_w"][l], 8)
        for k in range(4):
            pp[:, b + O_RCW + k * 8:b + O_RCW + k * 8 + 8] = fm(inp["rglru_conv_w"][l, k], 8)
            pp[:, b + O_SCW + k * 12:b + O_SCW + k * 12 + 12] = fm(inp["ssd_conv_w"][l, k], 12)
        pp[:, b + O_RCB:b + O_RCB + 8] = fm(inp["rglru_conv_b"][l], 8)
        pp[:, b + O_BA:b + O_BA + 8] = fm(inp["rglru_ba"][l].reshape(-1), 8)
        pp[:, b + O_BX:b + O_BX + 8] = fm(inp["rglru_bx"][l].reshape(-1), 8)
        pp[:, b + O_LAM:b + O_LAM + 8] = fm(inp["rglru_lambda"][l], 8)
        pp[:, b + O_SCB:b + O_SCB + 12] = fm(inp["ssd_conv_b"][l], 12)
        pp[:, b + O_SNW:b + O_SNW + 8] = fm(inp["ssd_norm_w"][l], 8)
        rb[:, l * RB_L + 0:l * RB_L + 16] = inp["ssd_dt_bias"][l][None, :]
        rb[:, l * RB_L + 16:l * RB_L + 32] = inp["ssd_a_log"][l][None, :]
        rb[:, l * RB_L + 32:l * RB_L + 48] = inp["ssd_d"][l][None, :]
    w_in = inp["w_in"]
    wmain = np.ascontiguousarray(
        w_in[:L, :, :COL_DT].reshape(L, KC, 128, NG_MAIN, G).transpose(0, 3, 2, 1, 4))
    wdt = np.ascontiguousarray(w_in[:L, :, COL_DT:COL_G].reshape(L, KC, 128, 16).transpose(0, 2, 1, 3))
    wgate = np.ascontiguousarray(
        w_in[:L, :, COL_G:].reshape(L, KC, 128, NG_GATE, G).transpose(0, 3, 2, 1, 4))
    wout = np.ascontiguousarray(
        inp["w_out"][:L].reshape(L, 3, 8, 128, 8, G).transpose(0, 1, 4, 3, 2, 5))
    wab = np.ascontiguousarray(np.stack([inp["rglru_wa"][:L], inp["rglru_wx"][:L]], 1).transpose(0, 3, 1, 2, 4))
    shared = dict(pp=pp, rb=rb, cst=host_consts(), wmain=wmain, wdt=wdt, wgate=wgate, wout=wout, wab=wab)
    x = inp["x"]
    percore = []
    for r in range(8):
        b, s = r // 4, r % 4
        xs = x[b, s * T:(s + 1) * T, :]
        xT = np.ascontiguousarray(xs.reshape(T, KC, 128).transpose(2, 1, 0))
        xh = np.zeros((128, KC, 4), f32)
        if s > 0:
            xh[:, :, 0:3] = x[b, s * T - 3:s * T, :].reshape(3, KC, 128).transpose(2, 1, 0)
        sel = np.zeros((128, 8), f32)
        for m in range(4):
            sel[:, m] = 1.0 if m < s else 0.0
            sel[:, 4 + m] = 1.0 if m == s - 1 else 0.0
        percore.append(dict(xT=xT, xh=xh, sel=sel))
    return shared, percore


class Prog:
    def __init__(self, L=4, branches="BAC", dbg=False, generic=False):
        self.generic = generic
        self.L = L
        self.branches = branches
        self.dbg = dbg
        self.stack = ExitStack()
        self.nc = bass.Bass("TRN2", target_bir_lowering=False)
        self.S = Sched(self.nc, self.stack)
        self.build()

    def build(self):
        S, nc, L = self.S, self.nc, self.L
        D = S.dram
        self.d_xT = D("xT", [128, KC, T], F32, kind="ExternalInput", track=False)
        self.d_xh = D("xh", [128, KC, 4], F32, kind="ExternalInput", track=False)
        self.d_sel = D("sel", [128, 8], F32, kind="ExternalInput", track=False)
        self.d_pp = D("pp", [128, NP], F32, kind="ExternalInput", track=False)
        self.d_rb = D("rb", [128, NR], F32, kind="ExternalInput", track=False)
        self.d_cst = D("cst", [128, NCST], F32, kind="ExternalInput", track=False)
        self.d_wmain = D("wmain", [L, NG_MAIN, 128, KC * G], F32, kind="ExternalInput", track=False)
        self.d_wdt = D("wdt", [L, 128, KC * 16], F32, kind="ExternalInput", track=False)
        self.d_wgate = D("wgate", [L, NG_GATE, 128, KC * G], F32, kind="ExternalInput", track=False)
        self.d_wout = D("wout", [L, 3, 8, 128, 8 * G], F32, kind="ExternalInput", track=False)
        self.d_wab = D("wab", [L, 128, 2 * 8 * 128], F32, kind="ExternalInput", track=False)
        self.d_out = D("out", [128, KC, T], F32, kind="ExternalOutput")
        self.d_xout = D("xout", [128, KC, T], F32, kind="ExternalOutput")
        self.d_mlb = D("mlb", [128, 4], F32, kind="ExternalInput", track=False)
        if self.dbg:
            self.d_dbg = D("dbg", [3, 128, 8, T], BF16, kind="ExternalOutput", blk=128 * 8 * T)

        self.xT = S.sb("xT_s", [128, KC, T], F32, blk=512)
        self.hT = S.sb("hT_s", [128, KC, T], BF16, blk=512)
        self.hh = S.sb("hh_s", [128, KC, 4], BF16)
        self.xh = S.sb("xh_s", [128, KC, 4], F32)
        self.oT = S.sb("oT_s", [128, 8, T], BF16, blk=512)
        self.NBUF = 2
        self.wbuf = [S.sb("wbuf%d" % i, [128, KC * G], BF16) for i in range(self.NBUF)]
        self.wslot = 0
        self.pp = S.sb("pp_s", [128, NP], F32)
        self.rb = S.sb("rb_s", [128, NR], F32)
        self.sel = S.sb("sel_s", [128, 8], F32)
        self.mlb = S.sb("mlb_s", [128, 4], F32)
        self.ident_bf = S.sb("ident_bf", [128, 128], BF16)
        self.ones_bf = S.sb("ones_bf", [128, 128], BF16)
        self.lb = S.sb("lb_s", [128, 4, 8], F32)
        self.omlb = S.sb("omlb_s", [128, 4, 8], F32)
        self.c8 = S.sb("c8_s", [128, 4, 8], F32)
        self.c16 = S.sb("c16_s", [128, 4, 8], F32)
        self.tiny = S.sb("tiny_s", [128, 256], F32, blk=32)
        self.NBLK = 62
        self.arena = S.sb("arena", [128, self.NBLK * 256], F32, blk=256)
        self.scr = [self.af(0, 5)[:, 0:1028]] + [self.af(5 + 4 * i, 4) for i in range(8)]
        self.bscr = [self.ab(37 + 2 * i, 2) for i in range(2)]
        self.oca = self.ab(41, 16).rearrange("p (h t) -> p h t", h=8)
        self.wab = self.ab(57, 4).rearrange("p (a h j) -> p a h j", a=2, h=8)
        self.cst = self.af(0, 2)
        self.cmask = S.sb("cmask_s", [128, T], U8)
        self.maskA = S.sb("maskA_s", [128, 128], U8)
        self.tri_bf = S.sb("tri_bf", [128, 128], BF16)
        self.ntri_bf = S.sb("ntri_bf", [128, 128], BF16)
        self.negrep = S.sb("negrep_s", [128, 4, 128], BF16)
        self.scTm = [S.sb("scTm%d" % i, [128, 128], BF16) for i in range(2)]
        self.Sst = S.sb("Sst_s", [128, 128], F32)
        self.Sbf = [S.sb("Sbf%d" % i, [128, 128], BF16) for i in range(2)]
        self.small = S.sb("small_s", [128, 1792], F32, blk=128)
        self.wdt_s = self.ab(61, 1)[:, 0:KC * 16]
        self.psum = S.ps("psum", [128, 4096], F32, blk=512)
        self.pbank = 0
        self.cc_id = 0

        ident_f = self.cst[:, C_ID:C_ID + 128]
        S.dma("sp", "ld_x", self.xT[:], self.d_xT)
        S.dma("sp", "ld_xh", self.xh[:], self.d_xh)
        S.dma("sp", "ld_pp", self.pp[:], self.d_pp)
        S.dma("sp", "ld_rb", self.rb[:], self.d_rb)
        S.dma("sp", "ld_sel", self.sel[:], self.d_sel)
        S.dma("sp", "ld_mlb", self.mlb[:], self.d_mlb)
        S.dma("sp", "ld_cst", self.cst[:], self.d_cst)
        S.copy(self.ident_bf[:], ident_f)
        S.memset(self.ones_bf[:], 1.0)
        S.memset(self.cmask[:], 1.0)
        S.memset(self.cmask[:].rearrange("p (c k) -> p c k", k=64)[:, :, 0:1], 0.0)
        S.copy(self.maskA[:], self.cst[:, C_MA:C_MA + 128])
        S.copy(self.tri_bf[:], self.cst[:, C_TRI:C_TRI + 128])
        S.ts(self.ntri_bf[:], self.cst[:, C_TRI:C_TRI + 128], -1.0, None, ALU.mult)
        S.copy(self.negrep[:], self.cst[:, C_NEG:C_NEG + 128].unsqueeze(1).to_broadcast([128, 4, 128]))
        for i in range(2):
            S.memset(self.scTm[i][:], 0.0)
        self.setup_params()
        for l in range(L):
            self.layer(l)
        self.final()
        S.wait_all("sp")
        S.emit()

    def setup_params(self):
        S = self.S
        tv = self.tiny
        lg = self.pp[:, PP_LBL:PP_LBL + 32].rearrange("p (l h) -> p l h", l=4)
        m = tv[:, 0:8]
        S.tt(m, lg[:, 0, :], lg[:, 1, :], ALU.max)
        S.tt(m, m, lg[:, 2, :], ALU.max)
        S.tt(m, m, lg[:, 3, :], ALU.max)
        e = tv[:, 32:64].rearrange("p (l h) -> p l h", l=4)
        S.tt(e, lg, m.unsqueeze(1).to_broadcast([128, 4, 8]), ALU.subtract)
        S.act(e, e, AF.Exp)
        ss = tv[:, 8:16]
        S.tt(ss, e[:, 0, :], e[:, 1, :], ALU.add)
        S.tt(ss, ss, e[:, 2, :], ALU.add)
        S.tt(ss, ss, e[:, 3, :], ALU.add)
        S.recip(ss, ss)
        S.tt(e, e, ss.unsqueeze(1).to_broadcast([128, 4, 8]), ALU.mult)
        S.memset(self.lb[:, 0, :], 0.0)
        S.copy(self.lb[:, 1, :], e[:, 1, :])
        S.tt(self.lb[:, 2, :], e[:, 1, :], e[:, 2, :], ALU.add)
        S.tt(self.lb[:, 3, :], self.lb[:, 2, :], e[:, 3, :], ALU.add)
        if self.generic:
            S.ts(self.lb[:, 0, :], e[:, 0, :], self.mlb[:, 0:1], None, ALU.mult)
            for i in range(1, 4):
                S.stt(self.lb[:, 0, :], e[:, i, :], self.mlb[:, i:i + 1], self.lb[:, 0, :], ALU.mult, ALU.add)
        S.ts(self.omlb[:], self.lb[:], -1.0, 1.0, ALU.mult, ALU.add)
        for l in range(self.L):
            lam = self.pp[:, PP_L0 + l * PL + O_LAM:PP_L0 + l * PL + O_LAM + 8]
            t = tv[:, 64:72]
            S.act(t, lam, AF.Exp, scale=-1.0)
            S.act(t, t, AF.Ln, bias=1.0)
            S.ts(self.c8[:, l, :], t, -8.0, None, ALU.mult)
            S.ts(self.c16[:, l, :], t, -16.0, None, ALU.mult)

    def dump(self, name, ap):
        if not self.dbg:
            return
        if not hasattr(self, "dumps"):
            self.dumps = {}
        if name in self.dumps:
            return
        d = self.S.dram("dump_" + name, list(ap.shape), ap.dtype, kind="ExternalOutput")
        self.dumps[name] = d
        self.S.dma("sp", "dump", d, ap)

    def af(self, b0, nb):
        return self.arena[:, b0 * 256:(b0 + nb) * 256]

    def ab(self, b0, nb):
        return self.arena[:, b0 * 256:(b0 + nb) * 256].bitcast(BF16)

    def P(self, l, off, n=1):
        b = PP_L0 + l * PL + off
        return self.pp[:, b:b + n]

    def bank(self, n=1):
        b = self.pbank
        if b + n > 6:
            b = 0
        self.pbank = (b + n) % 6
        return self.psum[:, b * 512:(b + n) * 512]

    def load_w(self, src, ncols_total):
        slot = self.wslot
        self.wslot = (self.wslot + 1) % self.NBUF
        dst = self.wbuf[slot][:, 0:ncols_total]
        self.S.dma("pool", "w%d" % slot, dst, src, max_dma_last_dim=8192)
        return self.wbuf[slot]

    def wmain(self, l, gi):
        w = self.load_w(self.d_wmain[l, gi], KC * G)
        return w[:, :].rearrange("p (k c) -> p k c", k=KC)

    def wgate(self, l, gi):
        w = self.load_w(self.d_wgate[l, gi], KC * G)
        return w[:, :].rearrange("p (k c) -> p k c", k=KC)

    def wout(self, l, n, g):
        w = self.load_w(self.d_wout[l, n, g], 8 * G)
        return w[:, 0:8 * G].rearrange("p (k c) -> p k c", k=8)

    def proj_fm(self, w, j, halo_ps=None):
        S = self.S
        ps = self.bank(2)
        hT, hh = self.hT, self.hh

        def fn(e):
            ins = None
            for half in range(2):
                for kc in range(KC):
                    ins = e.matmul(ps[:, half * 512:(half + 1) * 512], w[:, kc, j * 128:(j + 1) * 128],
                                   hT[:, kc, half * 512:(half + 1) * 512], start=(kc == 0), stop=(kc == KC - 1))
            if halo_ps is not None:
                for kc in range(KC):
                    ins = e.matmul(halo_ps, w[:, kc, j * 128:(j + 1) * 128], hh[:, kc, 0:4],
                                   start=(kc == 0), stop=(kc == KC - 1))
            return ins
        outs = [ps] + ([halo_ps] if halo_ps is not None else [])
        S.add("pe", fn, outs, [w[:, :, j * 128:(j + 1) * 128], hT[:], hh[:]])
        return ps

    def layer(self, l):
        S = self.S
        self.rmsnorm_in(l)
        if l > 0 or True:
            pass
        S.dma("pool", "ld_wab", self.ab(57, 4), self.d_wab[l], max_dma_last_dim=8192)
        for br in self.branches:
            if br == "B":
                self.branch_B(l)
                self.out_proj(l, 1)
            elif br == "A":
                self.branch_A(l)
                self.out_proj(l, 0)
            elif br == "C":
                self.branch_C(l)
                self.out_proj(l, 2)
        if l + 1 < self.L:
            self.halo_exchange(l)

    def rmsnorm_in(self, l):
        S = self.S
        xT, hT = self.xT, self.hT
        for half in range(2):
            sl = slice(half * 512, (half + 1) * 512)
            ps = self.bank(1)
            sqs = []
            for kc in range(KC):
                sq = self.bscr[kc % 2][:, 0:512]
                S.act(sq, xT[:, kc, sl], AF.Square)
                S.mm(ps, [(self.ones_bf[:], sq)], start=(kc == 0), stop=(kc == KC - 1))
            rstd = self.scr[0][:, 0:512]
            S.act(rstd, ps, AF.Sqrt, bias=EPS, scale=1.0 / 2048.0)
            S.recip(rstd, rstd)
            for kc in range(KC):
                S.stt(hT[:, kc, sl], xT[:, kc, sl], self.P(l, O_NW + kc), rstd, ALU.mult, ALU.mult)
        xh = self.xh
        sqh = self.bscr[0][:, 0:64].rearrange("p (k c) -> p k c", k=KC)
        S.act(sqh, xh[:], AF.Square)
        psh = self.psum[:, 7 * 512:7 * 512 + 4]
        S.mm(psh, [(self.ones_bf[:], sqh[:, kc, :]) for kc in range(KC)])
        rh = self.tiny[:, 96:100]
        S.act(rh, psh, AF.Sqrt, bias=EPS, scale=1.0 / 2048.0)
        S.recip(rh, rh)
        th = self.scr[0][:, 512:576].rearrange("p (k c) -> p k c", k=KC)
        S.tt(th, xh[:], self.P(l, O_NW, 16).unsqueeze(2).to_broadcast([128, KC, 4]), ALU.mult)
        S.tt(self.hh[:], th, rh.unsqueeze(1).to_broadcast([128, KC, 4]), ALU.mult)

    def exchange(self, src_sb, width, name):
        S = self.S
        i = self.cc_id
        self.cc_id += 1
        if not hasattr(self, "ccbufs"):
            self.ccbufs = {}
        if name not in self.ccbufs:
            self.ccbufs[name] = (S.dram("ccin_" + name, [128, width], F32),
                                 S.dram("ccout_" + name, [4 * 128, width], F32))
        bin_, bout = self.ccbufs[name]
        if isinstance(src_sb, list):
            for (src, off, w) in src_sb:
                S.dma("sp", "cc_st_" + name, bin_[:, off:off + w], src)
        else:
            S.dma("sp", "cc_st_" + name, bin_, src_sb)
        S.coll("cc%d" % i, "AllGather", [[0, 1, 2, 3], [4, 5, 6, 7]], bin_, bout)
        return bout

    def halo_exchange(self, l):
        S = self.S
        src = self.scr[1][:, 0:64]
        S.copy(src.rearrange("p (k c) -> p k c", k=KC), self.xT[:, :, T - 4:T])
        bout = self.exchange(src, 64, "halo")
        g = self.scr[2][:, 0:256]
        S.dma("sp", "cc_ld_halo", g.rearrange("p (r c) -> p r c", r=4), bout.rearrange("(r p) c -> p r c", p=128))
        gv = g.rearrange("p (r k c) -> p r k c", r=4, k=KC)
        xh = self.xh
        S.ts(xh[:, :, 0:3], gv[:, 0, :, 1:4], self.sel[:, 4:5], None, ALU.mult)
        for m in range(1, 4):
            S.stt(xh[:, :, 0:3], gv[:, m, :, 1:4], self.sel[:, 4 + m:5 + m], xh[:, :, 0:3], ALU.mult, ALU.add)

    def branch_B(self, l):
        S = self.S
        xb_, xc_, r_, ig_, a2_, hl_, ca_, sz_ = [self.scr[i] for i in range(8)]
        zero = self.scr[8][:, 0:T]
        S.memset(zero, 0.0)
        hlast = self.tiny[:, 128:136]
        atot = self.tiny[:, 136:144]
        wx_list = {}
        for h in range(8):
            gi, j = (COL_BX // G) + h // 2, h % 2
            if j == 0:
                wcur = self.wmain(l, gi)
            halo_ps = self.psum[:, 6 * 512:6 * 512 + 4]
            ps = self.proj_fm(wcur, j, halo_ps)
            xb = xb_[:, 0:T + 3]
            S.copy(xb[:, 3:T + 3], ps, eng="act")
            S.copy(xb[:, 0:3], halo_ps[:, 0:3], eng="act")
            xc = xc_[:, 0:T]
            S.ts(xc, xb[:, 0:T], self.P(l, O_RCW + 0 * 8 + h), self.P(l, O_RCB + h), ALU.mult, ALU.add)
            for k in range(1, 4):
                S.stt(xc, xb[:, k:k + T], self.P(l, O_RCW + k * 8 + h), xc, ALU.mult, ALU.add)
            xcb = self.bscr[0][:, 0:T]
            S.copy(xcb, xc, eng="act")
            if h == 0 and l == 0:
                self.dump("hT0", self.hT[:, 0, :]); self.dump("xb", xb); self.dump("xc", xc); self.dump("hh", self.hh[:])
            r = r_[:, 0:T]
            ig = ig_[:, 0:T]
            for (dst, which, boff) in ((r, 0, O_BA), (ig, 1, O_BX)):
                psg = self.bank(2)
                for half in range(2):
                    S.mm(psg[:, half * 512:(half + 1) * 512], [(self.wab[:, which, h, :], xcb[:, half * 512:(half + 1) * 512])])
                S.act(dst, psg, AF.Sigmoid, bias=self.P(l, boff + h))
            a2 = a2_[:, 0:T]
            S.act(a2, r, AF.Exp, scale=self.c16[:, l, h:h + 1])
            S.act(r, r, AF.Exp, scale=self.c8[:, l, h:h + 1])
            S.ts(a2, a2, -1.0, 1.0, ALU.mult, ALU.add)
            S.act(a2, a2, AF.Sqrt)
            S.tt(ig, ig, xc, ALU.mult)
            S.tt(a2, a2, ig, ALU.mult)
            hl = hl_[:, 0:T]
            ca = ca_[:, 0:T]
            if h == 0 and l == 0:
                self.dump("a", r); self.dump("u", a2); self.dump("igx", ig)
            S.scan(hl, r, a2, 0.0)
            S.scan(ca, r, zero, 1.0)
            S.copy(hlast[:, h:h + 1], hl[:, T - 1:T])
            S.copy(atot[:, h:h + 1], ca[:, T - 1:T])
            gi, j = (COL_BZ // G) + h // 2, h % 2
            if j == 0:
                wz = self.wmain(l, gi)
            psz = self.proj_fm(wz, j)
            sz = sz_[:, 0:T]
            S.act(sz, psz, AF.Silu)
            if h == 0 and l == 0:
                self.dump("hl", hl); self.dump("ca", ca); self.dump("sz", sz)
            S.tt(self.oT[:, h, :], hl, sz, ALU.mult)
            S.tt(self.oca[:, h, :], ca, sz, ALU.mult)
        src = self.tiny[:, 128:144]
        bout = self.exchange(src, 16, "B")
        g = self.tiny[:, 160:224]
        S.dma("sp", "cc_ld_B", g.rearrange("p (r c) -> p r c", r=4), bout.rearrange("(r p) c -> p r c", p=128))
        gv = g.rearrange("p (r c) -> p r c", r=4)
        hin = self.tiny[:, 144:152]
        self.horner(hin, lambda m: gv[:, m, 8:16], lambda m: gv[:, m, 0:8], 8)
        for h in range(8):
            S.stt(self.oT[:, h, :], self.oca[:, h, :], hin[:, h:h + 1], self.oT[:, h, :], ALU.mult, ALU.add)

    def horner(self, tacc, dec, val, width):
        S = self.S
        tmp = self.tiny[:, 224:224 + width]
        tmp2 = self.tiny[:, 240:240 + width]
        S.memset(tacc, 0.0)
        for m in range(3):
            pm = self.sel[:, m:m + 1]
            S.ts(tmp, dec(m), -1.0, pm, ALU.add, ALU.mult)
            S.ts(tmp, tmp, 1.0, None, ALU.add)
            S.tt(tacc, tacc, tmp, ALU.mult)
            S.ts(tmp2, val(m), pm, None, ALU.mult)
            S.tt(tacc, tacc, tmp2, ALU.add)

    def out_proj(self, l, n):
        S = self.S
        if self.dbg and l == 0:
            S.dma("sp", "dbg", self.d_dbg[n], self.oT[:])
        for g in range(8):
            wg = self.wgate(l, n * 8 + g)
            wo = self.wout(l, n, g)
            for j in range(2):
                dc = g * 2 + j
                for half in range(2):
                    sl = slice(half * 512, (half + 1) * 512)
                    psg = self.bank(1)
                    S.mm(psg, [(wg[:, kc, j * 128:(j + 1) * 128], self.hT[:, kc, sl]) for kc in range(KC)])
                    gate = self.scr[half][:, 0:512]
                    S.act(gate, psg, AF.Sigmoid)
                    pso = self.bank(1)
                    S.mm(pso, [(wo[:, cc, j * 128:(j + 1) * 128], self.oT[:, cc, sl]) for cc in range(8)])
                    S.tt(gate, pso, gate, ALU.mult)
                    S.tt(self.xT[:, dc, sl], self.xT[:, dc, sl], gate, ALU.add, eng="pool")

    def final(self):
        S = self.S
        xT = self.xT
        for kc in range(KC):
            S.dma("sp", "st_xout", self.d_xout[:, kc, :], xT[:, kc, :])
        for half in range(2):
            sl = slice(half * 512, (half + 1) * 512)
            ps = self.bank(1)
            for kc in range(KC):
                sq = self.bscr[kc % 2][:, 0:512]
                S.act(sq, xT[:, kc, sl], AF.Square)
                S.mm(ps, [(self.ones_bf[:], sq)], start=(kc == 0), stop=(kc == KC - 1))
            rstd = self.scr[0][:, 0:512]
            S.act(rstd, ps, AF.Sqrt, bias=EPS, scale=1.0 / 2048.0)
            S.recip(rstd, rstd)
            for kc in range(KC):
                S.stt(xT[:, kc, sl], xT[:, kc, sl], self.pp[:, PP_FNW + kc:PP_FNW + kc + 1], rstd, ALU.mult, ALU.mult)
        for kc in range(KC):
            S.dma("sp", "st_out", self.d_out[:, kc, :], xT[:, kc, :])

    def proj_tm(self, w, ncols, tc0, ps):
        hT = self.hT

        def fn(e):
            ins = None
            for i in range(2):
                tc = tc0 + i
                for kc in range(KC):
                    ins = e.matmul(ps[:, i * ncols:(i + 1) * ncols], hT[:, kc, tc * 128:(tc + 1) * 128],
                                   w[:, kc, 0:ncols], start=(kc == 0), stop=(kc == KC - 1))
            return ins
        self.S.add("pe", fn, [ps[:, 0:2 * ncols]], [w[:, :, 0:ncols], hT[:]])

    def branch_A(self, l):
        S = self.S
        vtok = self.ab(0, 16).rearrange("p (c v) -> p c v", c=8)
        qc = self.ab(16, 16).rearrange("p (h t) -> p h t", h=8)
        Lt, KK, Bc, BM, E2 = [self.af(32 + 4 * i, 4) for i in range(5)]
        qg, kg, kgT = [self.ab(52 + 2 * i, 2) for i in range(3)]
        stA = self.af(58, 4)
        dtotA = self.small[:, 416:424]
        sm = self.small
        for g in range(4):
            w = self.wmain(l, COL_AI // G + g)
            for tc0 in range(0, 8, 2):
                ps = self.bank(1)
                self.proj_tm(w, G, tc0, ps)
                S.copy(vtok[:, tc0:tc0 + 2, g * G:(g + 1) * G], ps.rearrange("p (i c) -> p i c", i=2), eng="act")
        c3 = lambda ap: ap.rearrange("p (c k) -> p c k", k=64)
        for h in range(8):
            j = h % 2
            if j == 0:
                wq = self.wmain(l, COL_AQ // G + h // 2)
                wf = self.wmain(l, COL_AF // G + h // 2)
            psf = self.proj_fm(wf, j)
            S.act(Lt, psf, AF.Sigmoid)
            S.act(KK, psf, AF.Sigmoid, scale=-1.0)
            psq = self.proj_fm(wq, j)
            S.ts(Lt, Lt, self.omlb[:, l, h:h + 1], self.lb[:, l, h:h + 1], ALU.mult, ALU.add)
            S.act(Lt, Lt, AF.Ln)
            S.scan(Bc, self.cmask[:], Lt, 0.0)
            S.tt(c3(BM), c3(Bc), c3(Bc)[:, :, 31:32].to_broadcast([128, 16, 64]), ALU.subtract)
            eR, eL, eT, eP, blast, incl, eRP = [sm[:, 16 * i:16 * i + 16] for i in range(7)]
            S.act(eR, c3(Bc)[:, :, 31], AF.Exp)
            S.act(eL, c3(BM)[:, :, 63], AF.Exp)
            S.act(eT, c3(Bc)[:, :, 63], AF.Exp)
            S.copy(blast, c3(Bc)[:, :, 63])
            S.scan(incl, sm[:, 128:144], blast, 0.0) if False else None
            S.memset(sm[:, 128:144], 1.0)
            S.scan(incl, sm[:, 128:144], blast, 0.0)
            S.act(dtotA[:, h:h + 1], incl[:, 15:16], AF.Exp)
            S.tt(eP, incl, blast, ALU.subtract)
            S.act(eP, eP, AF.Exp)
            S.tt(eRP, eR, eP, ALU.mult)
            S.act(E2, BM, AF.Exp, scale=-1.0)
            S.act(BM, BM, AF.Exp)
            S.tt(qg, psq, BM, ALU.mult)
            S.stt(kg, KK, self.omlb[:, l, h:h + 1], E2, ALU.mult, ALU.mult)
            S.tt(c3(qc[:, h, :]), c3(qg), eRP.unsqueeze(2).to_broadcast([128, 16, 64]), ALU.mult)
            pst = self.bank(1).bitcast(BF16)
            for tc in range(8):
                S.transpose(pst[:, tc * 128:(tc + 1) * 128], kg[:, tc * 128:(tc + 1) * 128], self.ident_bf[:])
            S.copy(kgT, pst, eng="act")
            Sst = self.Sst
            for tc in range(8):
                tsl = slice(tc * 128, (tc + 1) * 128)
                psS = self.bank(1)[:, 0:128]
                S.mm(psS, [(kg[:, tsl], qg[:, tsl])])
                sc = self.scTm[tc % 2]
                S.add("dve", lambda e, sc=sc, psS=psS: e.copy_predicated(out=sc[:], mask=self.maskA[:], data=psS),
                      [sc[:]], [self.maskA[:], psS])
                pso = self.bank(1)[:, 0:128]
                S.mm(pso, [(vtok[:, tc, h * 128:(h + 1) * 128], sc[:])], start=True, stop=False)
                for i in range(2):
                    c = 2 * tc + i
                    rows = slice(i * 64, (i + 1) * 64)
                    if c > 0:
                        sb_ = self.Sbf[c % 2]
                        S.ts(sb_[:], Sst[:], eR[:, c:c + 1], None, ALU.mult)
                        S.mm(pso[:, i * 64:(i + 1) * 64], [(sb_[:], qg[:, c * 64:(c + 1) * 64])], start=False, stop=(i == 1))
                    elif True:
                        pass
                    psU = self.bank(1)[:, 0:128]
                    S.mm(psU, [(kgT[rows, tc * 128:(tc + 1) * 128], vtok[rows, tc, h * 128:(h + 1) * 128])])
                    if c == 0:
                        S.ts(Sst[:], psU, eL[:, c:c + 1], None, ALU.mult)
                    else:
                        tmpU = sm[:, 256:384]
                        S.ts(tmpU, psU, eL[:, c:c + 1], None, ALU.mult)
                        S.stt(Sst[:], Sst[:], eT[:, c:c + 1], tmpU, ALU.mult, ALU.add)
                S.copy(self.oT[:, h, tsl], pso, eng="act")
            S.copy(stA[:, h * 128:(h + 1) * 128], Sst[:])
        bout = self.exchange([(stA, 0, 1024), (dtotA, 1024, 8)], 1032, "A")
        gth = self.af(32, 13)[:, 0:3 * 1032].rearrange("p (r c) -> p r c", r=3)
        S.dma("sp", "cc_ld_A", gth, bout.rearrange("(r p) c -> p r c", p=128)[:, 0:3, :])
        Tin = self.af(45, 4)
        S.memset(Tin, 0.0)
        dpr = sm[:, 400:408]
        for m in range(3):
            pm = self.sel[:, m:m + 1]
            S.ts(dpr, gth[:, m, 1024:1032], -1.0, pm, ALU.add, ALU.mult)
            S.ts(dpr, dpr, 1.0, None, ALU.add)
            S.tt(Tin.rearrange("p (h v) -> p h v", h=8), Tin.rearrange("p (h v) -> p h v", h=8),
                 dpr.unsqueeze(2).to_broadcast([128, 8, 128]), ALU.mult)
            S.stt(Tin, gth[:, m, 0:1024], pm, Tin, ALU.mult, ALU.add)
        Sin = self.ab(49, 2)
        S.copy(Sin, Tin, eng="act")
        O_, sz_, rs_ = self.af(53, 4), self.af(57, 4), self.af(32, 4)
        sqb = self.ab(51, 2)
        for h in range(8):
            j = h % 2
            if j == 0:
                wz = self.wmain(l, COL_AZ // G + h // 2)
            psz = self.proj_fm(wz, j)
            S.act(sz_, psz, AF.Silu)
            psc = self.bank(2)
            for half in range(2):
                sl = slice(half * 512, (half + 1) * 512)
                S.mm(psc[:, sl], [(Sin[:, h * 128:(h + 1) * 128], qc[:, h, sl])])
            S.tt(O_, psc, self.oT[:, h, :], ALU.add)
            S.act(sqb, O_, AF.Square)
            psn = self.bank(2)
            for half in range(2):
                sl = slice(half * 512, (half + 1) * 512)
                S.mm(psn[:, sl], [(self.ones_bf[:], sqb[:, sl])])
            S.act(rs_, psn, AF.Sqrt, bias=EPS, scale=1.0 / 128.0)
            S.recip(rs_, rs_)
            S.stt(O_, O_, self.P(l, O_HNW + h), rs_, ALU.mult, ALU.mult)
            S.tt(self.oT[:, h, :], O_, sz_, ALU.mult)

    def branch_C(self, l):
        S = self.S
        sm = self.small
        xs_tok = self.ab(0, 16).rearrange("p (c v) -> p c v", c=8)
        szt = self.ab(16, 16).rearrange("p (c v) -> p c v", c=8)
        BT = self.ab(32, 4).rearrange("p (g t) -> p g t", g=2)
        CT = self.ab(36, 4).rearrange("p (g t) -> p g t", g=2)
        Btok = self.ab(40, 4).rearrange("p (c g n) -> p c g n", c=8, g=2)
        xb = self.af(50, 5)[:, 0:T + 3]
        xc = self.af(55, 4)
        xsb = self.ab(59, 2)
        S.dma("pool", "ld_wdt", self.wdt_s, self.d_wdt[l], max_dma_last_dim=8192)
        wdt = self.wdt_s.rearrange("p (k c) -> p k c", k=KC)
        psdt = self.bank(1)[:, 0:128]
        hT = self.hT

        def fn(e):
            ins = None
            for tc in range(8):
                for kc in range(KC):
                    ins = e.matmul(psdt[:, tc * 16:(tc + 1) * 16], hT[:, kc, tc * 128:(tc + 1) * 128], wdt[:, kc, :],
                                   start=(kc == 0), stop=(kc == KC - 1))
            return ins
        S.add("pe", fn, [psdt], [wdt, hT[:]])
        rbl = self.rb[:, l * RB_L:(l + 1) * RB_L]
        v3 = lambda ap: ap.rearrange("p (c h) -> p c h", c=8)
        dt, da, dahi, cum, tot, ecum, decs, edec = [sm[:, 512 + 128 * i:640 + 128 * i] for i in range(8)]
        arep = sm[:, 448:464]
        S.tt(v3(dt), v3(psdt), rbl[:, 0:16].unsqueeze(1).to_broadcast([128, 8, 16]), ALU.add)
        S.act(dt, dt, AF.Exp)
        S.act(dt, dt, AF.Ln, bias=1.0)
        S.act(arep, rbl[:, 16:32], AF.Exp)
        S.ts(arep, arep, -1.0, None, ALU.mult)
        S.tt(v3(da), v3(dt), arep.unsqueeze(1).to_broadcast([128, 8, 16]), ALU.mult)
        dah_b = sm[:, 1536:1664].bitcast(BF16)[:, 0:128]
        dal_b = sm[:, 1536:1664].bitcast(BF16)[:, 128:256]
        S.copy(dah_b, da)
        S.tt(dahi, da, dah_b, ALU.subtract)
        S.copy(dal_b, dahi)
        pcum = self.bank(1)
        S.mm(pcum[:, 0:128], [(self.tri_bf[:], dah_b), (self.tri_bf[:], dal_b)])
        S.mm(pcum[:, 128:256], [(self.ones_bf[:], dah_b), (self.ones_bf[:], dal_b)])
        S.copy(cum, pcum[:, 0:128], eng="act")
        S.copy(tot, pcum[:, 128:256], eng="act")
        S.act(ecum, cum, AF.Exp)
        S.tt(decs, tot, cum, ALU.subtract)
        S.act(decs, decs, AF.Exp)
        S.act(edec, tot, AF.Exp)
        for g in range(4):
            w = self.wmain(l, COL_CZ // G + g)
            for tc0 in range(0, 8, 2):
                ps = self.bank(1)
                self.proj_tm(w, G, tc0, ps)
                S.act(szt[:, tc0:tc0 + 2, g * G:(g + 1) * G], ps.rearrange("p (i c) -> p i c", i=2), AF.Silu)
        for c in range(12):
            j = c % 2
            if j == 0:
                wx = self.wmain(l, COL_CX // G + c // 2)
            halo_ps = self.psum[:, 6 * 512:6 * 512 + 4]
            ps = self.proj_fm(wx, j, halo_ps)
            S.copy(xb[:, 3:T + 3], ps, eng="act")
            S.copy(xb[:, 0:3], halo_ps[:, 0:3], eng="act")
            S.ts(xc, xb[:, 0:T], self.P(l, O_SCW + 0 * 12 + c), self.P(l, O_SCB + c), ALU.mult, ALU.add)
            for k in range(1, 4):
                S.stt(xc, xb[:, k:k + T], self.P(l, O_SCW + k * 12 + c), xc, ALU.mult, ALU.add)
            if c < 8:
                dst = xsb
            elif c < 10:
                dst = BT[:, c - 8, :]
            else:
                dst = CT[:, c - 10, :]
            S.act(dst, xc, AF.Silu)
            if c < 10:
                pst = self.bank(1).bitcast(BF16)
                for tc in range(8):
                    S.transpose(pst[:, tc * 128:(tc + 1) * 128], dst[:, tc * 128:(tc + 1) * 128], self.ident_bf[:])
                pv = pst.rearrange("p (c v) -> p c v", c=8)
                if c < 8:
                    S.copy(xs_tok[:, :, c * 128:(c + 1) * 128], pv, eng="act")
                else:
                    S.copy(Btok[:, :, c - 8, :], pv, eng="act")
        stT = self.af(44, 4)
        stb = self.ab(48, 2)
        eg = self.ab(50, 4)
        dab = self.ab(54, 2)
        xdt = self.ab(56, 2)
        y = self.af(58, 4)
        cbT = sm[:, 1664:1792].bitcast(BF16)
        h3 = lambda ap: ap.rearrange("p (h q) -> p h q", h=16)

        def state_step(tc, first):
            S.tt(h3(xdt), h3(xdt), decs[:, tc * 16:(tc + 1) * 16].unsqueeze(2).to_broadcast([128, 16, 64]), ALU.mult)
            psn = self.bank(2)
            for g in range(2):
                S.mm(psn[:, g * 512:(g + 1) * 512], [(Btok[:, tc, g, :], xdt[:, g * 512:(g + 1) * 512])])
            if not first:
                S.tt(h3(stT), h3(stT), edec[:, tc * 16:(tc + 1) * 16].unsqueeze(2).to_broadcast([128, 16, 64]), ALU.mult)
                S.tt(stT, stT, psn, ALU.add)
            else:
                S.copy(stT, psn)

        def make_xdt(tc):
            S.tt(h3(xdt), h3(xs_tok[:, tc, :]), dt[:, tc * 16:(tc + 1) * 16].unsqueeze(2).to_broadcast([128, 16, 64]), ALU.mult)

        for tc in range(8):
            make_xdt(tc)
            state_step(tc, tc == 0)
        dtotC = sm[:, 424:440]
        tsum = sm[:, 464:480]
        S.tt(tsum, tot[:, 0:16], tot[:, 16:32], ALU.add)
        for tc in range(2, 8):
            S.tt(tsum, tsum, tot[:, tc * 16:(tc + 1) * 16], ALU.add)
        S.act(dtotC, tsum, AF.Exp)
        bout = self.exchange([(stT, 0, 1024), (dtotC, 1024, 16)], 1040, "C")
        gth = self.af(49, 13)[:, 0:3 * 1040].rearrange("p (r c) -> p r c", r=3)
        S.dma("sp", "cc_ld_C", gth, bout.rearrange("(r p) c -> p r c", p=128)[:, 0:3, :])
        S.memset(stT, 0.0)
        dpr = sm[:, 480:496]
        for m in range(3):
            pm = self.sel[:, m:m + 1]
            S.ts(dpr, gth[:, m, 1024:1040], -1.0, pm, ALU.add, ALU.mult)
            S.ts(dpr, dpr, 1.0, None, ALU.add)
            S.tt(h3(stT), h3(stT), dpr.unsqueeze(2).to_broadcast([128, 16, 64]), ALU.mult)
            S.stt(stT, gth[:, m, 0:1024], pm, stT, ALU.mult, ALU.add)
        drep = rbl[:, 32:48]
        for tc in range(8):
            tsl = slice(tc * 128, (tc + 1) * 128)
            S.copy(stb, stT, eng="act")
            dav = lambda t: t[:, tc * 16:(tc + 1) * 16]
            dab3 = dab.rearrange("p (a h l) -> p a h l", a=2, h=4)
            seg = self.psum[:, 2048:4096]
            for q in range(4):
                hs = slice(q * 4, (q + 1) * 4)
                S.copy(dab3[:, 0, :, :], dav(dah_b)[:, hs].unsqueeze(2).to_broadcast([128, 4, 128]))
                S.copy(dab3[:, 1, :, :], dav(dal_b)[:, hs].unsqueeze(2).to_broadcast([128, 4, 128]))
                bank = seg[:, q * 512:(q + 1) * 512]

                def fn(e, bank=bank):
                    e.matmul(bank, self.ntri_bf[:], dab3[:, 0, :, :], start=True, stop=False)
                    e.matmul(bank, self.ntri_bf[:], dab3[:, 1, :, :], start=False, stop=False)
                    ins = e.matmul(bank, self.ident_bf[:], self.negrep[:, :, :], start=False, stop=False)
                    for hh in range(4):
                        for a in range(2):
                            ins = e.matmul(bank[:, hh * 128:(hh + 1) * 128], dab3[:, a, hh, :], self.tri_bf[:],
                                           start=False, stop=(hh == 3 and a == 1))
                    return ins
                S.add("pe", fn, [bank], [dab, self.ntri_bf[:], self.tri_bf[:], self.negrep[:], self.ident_bf[:]])
                S.act(eg[:, q * 512:(q + 1) * 512], bank, AF.Exp)
            pscb = self.bank(1)[:, 0:256]
            for g in range(2):
                S.mm(pscb[:, g * 128:(g + 1) * 128], [(BT[:, g, tsl], CT[:, g, tsl])])
            S.copy(cbT, pscb, eng="act")
            eg4 = eg.rearrange("p (g h l) -> p g h l", g=2, h=8)
            for g in range(2):
                S.tt(eg4[:, g, :, :], eg4[:, g, :, :], cbT[:, g * 128:(g + 1) * 128].unsqueeze(1).to_broadcast([128, 8, 128]), ALU.mult)
            make_xdt(tc)
            psy = self.bank(2)
            eg3 = eg.rearrange("p (h l) -> p h l", h=16)

            def fny(e, psy=psy):
                ins = None
                for hh in range(16):
                    ins = e.matmul(psy[:, hh * 64:(hh + 1) * 64], eg3[:, hh, :], xdt[:, hh * 64:(hh + 1) * 64],
                                   start=True, stop=True)
                return ins
            S.add("pe", fny, [psy], [eg, xdt])
            psyo = self.bank(2)
            for g in range(2):
                S.mm(psyo[:, g * 512:(g + 1) * 512], [(CT[:, g, tsl], stb[:, g * 512:(g + 1) * 512])])
            S.tt(h3(y), h3(xs_tok[:, tc, :]), drep.unsqueeze(2).to_broadcast([128, 16, 64]), ALU.mult)
            S.tt(y, y, psy, ALU.add)
            yo = self.af(50, 4)
            S.tt(h3(yo), h3(psyo), ecum[:, tc * 16:(tc + 1) * 16].unsqueeze(2).to_broadcast([128, 16, 64]), ALU.mult)
            S.tt(y, y, yo, ALU.add)
            state_step(tc, False)
            S.tt(y, y, szt[:, tc, :], ALU.mult)
            ss = sm[:, 496:498]
            junk = self.af(50, 2)
            for g in range(2):
                S.act(junk[:, 0:512], y[:, g * 512:(g + 1) * 512], AF.Square, accum_out=ss[:, g:g + 1])
            S.act(ss, ss, AF.Sqrt, bias=EPS, scale=1.0 / 512.0)
            S.recip(ss, ss)
            ot = self.ab(52, 2)
            for g in range(2):
                S.ts(ot[:, g * 512:(g + 1) * 512], y[:, g * 512:(g + 1) * 512], ss[:, g:g + 1], None, ALU.mult)
            pst = self.bank(1).bitcast(BF16)
            for cc in range(8):
                S.transpose(pst[:, cc * 128:(cc + 1) * 128], ot[:, cc * 128:(cc + 1) * 128], self.ident_bf[:])
            S.tt(self.oT[:, :, tsl], pst.rearrange("p (c t) -> p c t", c=8),
                 self.P(l, O_SNW, 8).unsqueeze(2).to_broadcast([128, 8, 128]), ALU.mult)


def _to_fm(xs):
    return np.ascontiguousarray(xs.reshape(xs.shape[0], KC, 128).transpose(2, 1, 0))


def host_prep_layer(inp, l, x):
    f32 = np.float32
    one = {k: (np.asarray(v)[l:l + 1] if k not in ("x", "final_norm_w", "hgrn_lb_logits") else np.asarray(v))
           for k, v in inp.items()}
    pp = np.zeros((128, NP), f32)
    lbl = np.asarray(inp["hgrn_lb_logits"]).astype(f32)
    pp[:, PP_LBL:PP_LBL + 32] = lbl.reshape(4, 8, 128).transpose(2, 0, 1).reshape(128, 32)
    pp[:, PP_FNW:PP_FNW + 16] = fm(np.asarray(inp["final_norm_w"]).astype(f32), 16)
    rb = np.zeros((128, NR), f32)
    b = PP_L0
    pp[:, b + O_NW:b + O_NW + 16] = fm(one["norm_w"][0], 16)
    pp[:, b + O_HNW:b + O_HNW + 8] = fm(one["hgrn_norm_w"][0], 8)
    for k in range(4):
        pp[:, b + O_RCW + k * 8:b + O_RCW + k * 8 + 8] = fm(one["rglru_conv_w"][0, k], 8)
        pp[:, b + O_SCW + k * 12:b + O_SCW + k * 12 + 12] = fm(one["ssd_conv_w"][0, k], 12)
    pp[:, b + O_RCB:b + O_RCB + 8] = fm(one["rglru_conv_b"][0], 8)
    pp[:, b + O_BA:b + O_BA + 8] = fm(one["rglru_ba"][0].reshape(-1), 8)
    pp[:, b + O_BX:b + O_BX + 8] = fm(one["rglru_bx"][0].reshape(-1), 8)
    pp[:, b + O_LAM:b + O_LAM + 8] = fm(one["rglru_lambda"][0], 8)
    pp[:, b + O_SCB:b + O_SCB + 12] = fm(one["ssd_conv_b"][0], 12)
    pp[:, b + O_SNW:b + O_SNW + 8] = fm(one["ssd_norm_w"][0], 8)
    rb[:, 0:16] = one["ssd_dt_bias"][0][None, :]
    rb[:, 16:32] = one["ssd_a_log"][0][None, :]
    rb[:, 32:48] = one["ssd_d"][0][None, :]
    w_in = one["w_in"]
    wmain = np.ascontiguousarray(w_in[:, :, :COL_DT].reshape(1, KC, 128, NG_MAIN, G).transpose(0, 3, 2, 1, 4))
    wdt = np.ascontiguousarray(w_in[:, :, COL_DT:COL_G].reshape(1, KC, 128, 16).transpose(0, 2, 1, 3))
    wgate = np.ascontiguousarray(w_in[:, :, COL_G:].reshape(1, KC, 128, NG_GATE, G).transpose(0, 3, 2, 1, 4))
    wout = np.ascontiguousarray(one["w_out"].reshape(1, 3, 8, 128, 8, G).transpose(0, 1, 4, 3, 2, 5))
    wab = np.ascontiguousarray(np.stack([one["rglru_wa"], one["rglru_wx"]], 1).transpose(0, 3, 1, 2, 4))
    mlb = np.zeros((128, 4), f32)
    mlb[:, 1:l + 1] = 1.0
    shared = dict(pp=pp, rb=rb, cst=host_consts(), mlb=mlb,
                  wmain=wmain.reshape(1, NG_MAIN, 128, KC * G), wdt=wdt.reshape(1, 128, KC * 16),
                  wgate=wgate.reshape(1, NG_GATE, 128, KC * G), wout=wout.reshape(1, 3, 8, 128, 8 * G),
                  wab=wab.reshape(1, 128, 2 * 8 * 128))
    percore = []
    for r in range(8):
        bb, s = r // 4, r % 4
        xT = _to_fm(x[bb, s * T:(s + 1) * T, :])
        xh = np.zeros((128, KC, 4), f32)
        if s > 0:
            xh[:, :, 0:3] = x[bb, s * T - 3:s * T, :].reshape(3, KC, 128).transpose(2, 1, 0)
        sel = np.zeros((128, 8), f32)
        for m in range(4):
            sel[:, m] = 1.0 if m < s else 0.0
            sel[:, 4 + m] = 1.0 if m == s - 1 else 0.0
        percore.append(dict(xT=xT, xh=xh, sel=sel))
    return shared, percore


_PROG = {}


def kernel(**inputs):
    inp = {k: np.asarray(v) for k, v in inputs.items()}
    x = np.ascontiguousarray(inp["x"], dtype=np.float32)
    if "gen1" not in _PROG:
        _PROG["gen1"] = Prog(L=1, branches="BAC", dbg=False, generic=True)
    prog = _PROG["gen1"]
    out = None
    for l in range(4):
        shared, percore = host_prep_layer(inp, l, x)
        in_maps = [dict(shared, **pc) for pc in percore]
        res = run_bass_kernel_spmd(prog.nc, in_maps, core_ids=list(range(8)))
        key = "out" if l == 3 else "xout"
        xn = np.zeros_like(x)
        for r in range(8):
            bb, s = r // 4, r % 4
            o = np.asarray(res.results[r][key])
            xn[bb, s * T:(s + 1) * T, :] = o.transpose(2, 1, 0).reshape(T, 2048)
        x = xn
        out = xn
    return out
```

```python
import numpy as np
import concourse.bass as bass
import concourse.mybir as mybir

F32 = mybir.dt.float32
BF16 = mybir.dt.bfloat16
AF = mybir.ActivationFunctionType
ALU = mybir.AluOpType


def is_ap(x):
    return hasattr(x, "ap") and hasattr(x, "tensor") and hasattr(x, "offset")


class Sched:
    ENGS = ["pe", "act", "dve", "pool", "sp"]

    def __init__(self, nc, stack):
        self.nc = nc
        self.stack = stack
        self.ops = {e: [] for e in self.ENGS}
        self.sem = {}
        self.cnt = {}
        self.seen = {e: {} for e in self.ENGS}
        self.state = {}
        self.tinfo = {}
        for e in self.ENGS:
            self.newsem("E:" + e)
        self.nops = 0

    def newsem(self, key):
        if key not in self.sem:
            self.sem[key] = self.stack.enter_context(self.nc.semaphore("s%d" % len(self.sem)))
            self.cnt[key] = 0
        return self.sem[key]

    def sb(self, name, shape, dtype, blk=None):
        t = self.stack.enter_context(self.nc.sbuf_tensor(name, list(shape), dtype))
        psize = int(np.prod(shape[1:]))
        self.tinfo[name] = (psize, blk or psize, mybir.dt.size(dtype))
        return t

    def ps(self, name, shape, dtype, blk=None):
        t = self.stack.enter_context(self.nc.psum_tensor(name, list(shape), dtype))
        psize = int(np.prod(shape[1:]))
        self.tinfo[name] = (psize, blk or psize, mybir.dt.size(dtype))
        return t

    def dram(self, name, shape, dtype, kind=None, blk=None, track=True):
        if kind is None:
            t = self.nc.dram_tensor(name, list(shape), dtype)
        else:
            t = self.nc.dram_tensor(name, list(shape), dtype, kind=kind)
        if track:
            self.tinfo[name] = (None, blk or int(np.prod(shape)), mybir.dt.size(dtype))
        return t.ap()

    def _blocks(self, ap):
        name = ap.name
        info = self.tinfo.get(name)
        if info is None:
            return []
        psize, blk, esz = info
        off = int(ap.offset)
        pairs = ap.ap
        vsz = mybir.dt.size(ap.dtype)
        if psize is not None:
            vps = psize * esz // vsz
            lo = off % vps
            hi = lo + sum((c - 1) * abs(s) for s, c in pairs[1:])
        else:
            lo = off
            hi = lo + sum((c - 1) * abs(s) for s, c in pairs)
        lo = lo * vsz // esz
        hi = hi * vsz // esz
        return [(name, b) for b in range(lo // blk, hi // blk + 1)]

    def _deps(self, eng, outs, ins):
        waits = {}
        me = "E:" + eng
        seen = self.seen[eng]

        def need(ev):
            if ev is None:
                return
            sid, val = ev
            if sid == me and eng == "pe":
                return
            if seen.get(sid, 0) >= val:
                return
            if waits.get(sid, 0) < val:
                waits[sid] = val

        for ap in ins:
            for b in self._blocks(ap):
                st = self.state.get(b)
                if st:
                    need(st[0])
        for ap in outs:
            for b in self._blocks(ap):
                st = self.state.get(b)
                if st:
                    need(st[0])
                    for sid, val in st[1].items():
                        need((sid, val))
        for sid, val in waits.items():
            seen[sid] = val
        return list(waits.items())

    def _record(self, ev, outs, ins):
        sid, val = ev
        for ap in ins:
            for b in self._blocks(ap):
                st = self.state.setdefault(b, [None, {}])
                if st[1].get(sid, 0) < val:
                    st[1][sid] = val
        for ap in outs:
            for b in self._blocks(ap):
                self.state[b] = [ev, {}]

    def add(self, eng, fn, outs, ins):
        outs = [a for a in outs if is_ap(a)]
        ins = [a for a in ins if is_ap(a)]
        waits = self._deps(eng, outs, ins)
        sid = "E:" + eng
        self.cnt[sid] += 1
        ev = (sid, self.cnt[sid])
        self._record(ev, outs, ins)
        self.ops[eng].append((waits, fn, sid, 1))
        self.nops += 1

    def dma(self, queue, semkey, out, in_, **kw):
        self.newsem(semkey)
        waits = self._deps(queue, [out], [in_])
        self.cnt[semkey] += 16
        ev = (semkey, self.cnt[semkey])
        self._record(ev, [out], [in_])
        self.ops[queue].append((waits, lambda e: e.dma_start(out=out, in_=in_, **kw), semkey, 16))
        self.nops += 1

    def coll(self, semkey, kind, groups, in_ap, out_ap):
        self.newsem(semkey)
        assert self.cnt[semkey] == 0
        waits = self._deps("pool", [out_ap], [in_ap])
        self.cnt[semkey] = 1
        ev = (semkey, 1)
        self._record(ev, [out_ap], [in_ap])
        self.ops["pool"].append((waits, lambda e: e.collective_compute(
            kind, ALU.bypass, replica_groups=groups, ins=[in_ap.opt()], outs=[out_ap.opt()]), semkey, None))
        self.nops += 1

    def wait_all(self, eng):
        waits = []
        for sid, val in self.cnt.items():
            if sid == "E:" + eng or val == 0:
                continue
            if self.seen[eng].get(sid, 0) < val:
                waits.append((sid, val))
                self.seen[eng][sid] = val
        self.ops[eng].append((waits, None, None, 0))

    def emit(self):
        nc = self.nc
        with nc.Block() as block:
            def mk(engname):
                def body(e):
                    for waits, fn, sid, inc in self.ops[engname]:
                        for wsid, wval in waits:
                            e.wait_ge(self.sem[wsid], wval)
                        if fn is None:
                            continue
                        ins = fn(e)
                        if inc is None:
                            ins.then_inc(self.sem[sid])
                        else:
                            ins.then_inc(self.sem[sid], inc)
                return body
            block.tensor(mk("pe"))
            block.scalar(mk("act"))
            block.vector(mk("dve"))
            block.gpsimd(mk("pool"))
            block.sync(mk("sp"))

    def act(self, out, in_, func, bias=None, scale=None, accum_out=None):
        kw = {}
        if bias is not None:
            kw["bias"] = bias
        if scale is not None:
            kw["scale"] = scale
        if accum_out is not None:
            kw["accum_out"] = accum_out
        self.add("act", lambda e: e.activation(out=out, in_=in_, func=func, **kw),
                 [out, accum_out], [in_, bias, scale])

    def tt(self, out, in0, in1, op, eng="dve"):
        self.add(eng, lambda e: e.tensor_tensor(out=out, in0=in0, in1=in1, op=op), [out], [in0, in1])

    def ts(self, out, in0, s1, s2=None, op0=ALU.mult, op1=None, eng="dve", accum_out=None):
        kw = {}
        if op1 is not None:
            kw["op1"] = op1
        if accum_out is not None:
            kw["accum_out"] = accum_out
        self.add(eng, lambda e: e.tensor_scalar(out=out, in0=in0, scalar1=s1, scalar2=s2, op0=op0, **kw),
                 [out, accum_out], [in0, s1, s2])

    def stt(self, out, in0, scalar, in1, op0, op1, eng="dve"):
        self.add(eng, lambda e: e.scalar_tensor_tensor(out=out, in0=in0, scalar=scalar, in1=in1, op0=op0, op1=op1),
                 [out], [in0, scalar, in1])

    def copy(self, out, in_, eng="dve"):
        if eng == "act":
            self.add("act", lambda e: e.copy(out=out, in_=in_), [out], [in_])
        else:
            self.add(eng, lambda e: e.tensor_copy(out=out, in_=in_), [out], [in_])

    def memset(self, ap, val, eng="dve"):
        self.add(eng, lambda e: e.memset(ap, val), [ap], [])

    def recip(self, out, in_):
        self.add("dve", lambda e: e.reciprocal(out=out, in_=in_), [out], [in_])

    def scan(self, out, d0, d1, initial, op0=ALU.mult, op1=ALU.add):
        self.add("dve", lambda e: e.tensor_tensor_scan(out=out, data0=d0, data1=d1, initial=initial, op0=op0, op1=op1),
                 [out], [d0, d1, initial])

    def mm(self, out, pairs, start=True, stop=True):
        pairs = list(pairs)
        n = len(pairs)

        def fn(e):
            ins = None
            for i, (l, r) in enumerate(pairs):
                ins = e.matmul(out, l, r, start=(start and i == 0), stop=(stop and i == n - 1))
            return ins
        self.add("pe", fn, [out], [a for p in pairs for a in p])

    def transpose(self, out, in_, ident):
        self.add("pe", lambda e: e.transpose(out, in_, ident), [out], [in_, ident])


from contextlib import ExitStack
from concourse.bass_utils import run_bass_kernel_spmd

U8 = mybir.dt.uint8
T = 1024
KC = 16
G = 256
NG_MAIN = 34
NG_GATE = 24
COL_AQ, COL_AF, COL_AI, COL_AZ, COL_BX, COL_BZ, COL_CZ, COL_CX, COL_DT, COL_G = (
    0, 1024, 2048, 3072, 4096, 5120, 6144, 7168, 8704, 8720)
EPS = 1e-6

PP_LBL, PP_FNW, PP_L0 = 0, 32, 48
PL = 156
O_NW, O_HNW, O_RCW, O_RCB, O_BA, O_BX, O_LAM, O_SCW, O_SCB, O_SNW = 0, 16, 24, 56, 64, 72, 80, 88, 136, 148
NP = PP_L0 + 4 * PL
RB_L = 48
NR = 4 * RB_L
C_ID, C_TRI, C_MA, C_NEG, NCST = 0, 128, 256, 384, 512


def fm(v, n):
    return np.ascontiguousarray(v.reshape(n, 128).T)


def host_consts():
    c = np.zeros((128, NCST), np.float32)
    i = np.arange(128)
    c[:, C_ID:C_ID + 128] = np.eye(128, dtype=np.float32)
    c[:, C_TRI:C_TRI + 128] = (i[:, None] <= i[None, :]).astype(np.float32)
    c[:, C_MA:C_MA + 128] = ((i[:, None] <= i[None, :]) & (i[:, None] // 64 == i[None, :] // 64)).astype(np.float32)
    c[:, C_NEG:C_NEG + 128] = np.where(i[None, :] < i[:, None], -30000.0, 0.0).astype(np.float32)
    return c


def host_prep(inp, L=4):
    f32 = np.float32
    pp = np.zeros((128, NP), f32)
    lbl = inp["hgrn_lb_logits"].astype(f32)
    pp[:, PP_LBL:PP_LBL + 32] = lbl.reshape(4, 8, 128).transpose(2, 0, 1).reshape(128, 32)
    pp[:, PP_FNW:PP_FNW + 16] = fm(inp["final_norm_w"].astype(f32), 16)
    rb = np.zeros((128, NR), f32)
    for l in range(4):
        b = PP_L0 + l * PL
        pp[:, b + O_NW:b + O_NW + 16] = fm(inp["norm_w"][l], 16)
        pp[:, b + O_HNW:b + O_HNW + 8] = fm(inp["hgrn_norm_w"][l], 8)
        for k in range(4):
            pp[:, b + O_RCW + k * 8:b + O_RCW + k * 8 + 8] = fm(inp["rglru_conv_w"][l, k], 8)
            pp[:, b + O_SCW + k * 12:b + O_SCW + k * 12 + 12] = fm(inp["ssd_conv_w"][l, k], 12)
        pp[:, b + O_RCB:b + O_RCB + 8] = fm(inp["rglru_conv_b"][l], 8)
        pp[:, b + O_BA:b + O_BA + 8] = fm(inp["rglru_ba"][l].reshape(-1), 8)
        pp[:, b + O_BX:b + O_BX + 8] = fm(inp["rglru_bx"][l].reshape(-1), 8)
        pp[:, b + O_LAM:b + O_LAM + 8] = fm(inp["rglru_lambda"][l], 8)
        pp[:, b + O_SCB:b + O_SCB + 12] = fm(inp["ssd_conv_b"][l], 12)
        pp[:, b + O_SNW:b + O_SNW + 8] = fm(inp["ssd_norm_w"][l], 8)
        rb[:, l * RB_L + 0:l * RB_L + 16] = inp["ssd_dt_bias"][l][None, :]
        rb[:, l * RB_L + 16:l * RB_L + 32] = inp["ssd_a_log"][l][None, :]
        rb[:, l * RB_L + 32:l * RB_L + 48] = inp["ssd_d"][l][None, :]
    w_in = inp["w_in"]
    wmain = np.ascontiguousarray(
        w_in[:L, :, :COL_DT].reshape(L, KC, 128, NG_MAIN, G).transpose(0, 3, 2, 1, 4))
    wdt = np.ascontiguousarray(w_in[:L, :, COL_DT:COL_G].reshape(L, KC, 128, 16).transpose(0, 2, 1, 3))
    wgate = np.ascontiguousarray(
        w_in[:L, :, COL_G:].reshape(L, KC, 128, NG_GATE, G).transpose(0, 3, 2, 1, 4))
    wout = np.ascontiguousarray(
        inp["w_out"][:L].reshape(L, 3, 8, 128, 8, G).transpose(0, 1, 4, 3, 2, 5))
    wab = np.ascontiguousarray(np.stack([inp["rglru_wa"][:L], inp["rglru_wx"][:L]], 1).transpose(0, 3, 1, 2, 4))
    shared = dict(pp=pp, rb=rb, cst=host_consts(), wmain=wmain, wdt=wdt, wgate=wgate, wout=wout, wab=wab)
    x = inp["x"]
    percore = []
    for r in range(8):
        b, s = r // 4, r % 4
        xs = x[b, s * T:(s + 1) * T, :]
        xT = np.ascontiguousarray(xs.reshape(T, KC, 128).transpose(2, 1, 0))
        xh = np.zeros((128, KC, 4), f32)
        if s > 0:
            xh[:, :, 0:3] = x[b, s * T - 3:s * T, :].reshape(3, KC, 128).transpose(2, 1, 0)
        sel = np.zeros((128, 8), f32)
        for m in range(4):
            sel[:, m] = 1.0 if m < s else 0.0
            sel[:, 4 + m] = 1.0 if m == s - 1 else 0.0
        percore.append(dict(xT=xT, xh=xh, sel=sel))
    return shared, percore


class Prog:
    def __init__(self, L=4, branches="BAC", dbg=False, generic=False):
        self.generic = generic
        self.L = L
        self.branches = branches
        self.dbg = dbg
        self.stack = ExitStack()
        self.nc = bass.Bass("TRN2", target_bir_lowering=False)
        self.S = Sched(self.nc, self.stack)
        self.build()

    def build(self):
        S, nc, L = self.S, self.nc, self.L
        D = S.dram
        self.d_xT = D("xT", [128, KC, T], F32, kind="ExternalInput", track=False)
        self.d_xh = D("xh", [128, KC, 4], F32, kind="ExternalInput", track=False)
        self.d_sel = D("sel", [128, 8], F32, kind="ExternalInput", track=False)
        self.d_pp = D("pp", [128, NP], F32, kind="ExternalInput", track=False)
        self.d_rb = D("rb", [128, NR], F32, kind="ExternalInput", track=False)
        self.d_cst = D("cst", [128, NCST], F32, kind="ExternalInput", track=False)
        self.d_wmain = D("wmain", [L, NG_MAIN, 128, KC * G], F32, kind="ExternalInput", track=False)
        self.d_wdt = D("wdt", [L, 128, KC * 16], F32, kind="ExternalInput", track=False)
        self.d_wgate = D("wgate", [L, NG_GATE, 128, KC * G], F32, kind="ExternalInput", track=False)
        self.d_wout = D("wout", [L, 3, 8, 128, 8 * G], F32, kind="ExternalInput", track=False)
        self.d_wab = D("wab", [L, 128, 2 * 8 * 128], F32, kind="ExternalInput", track=False)
        self.d_out = D("out", [128, KC, T], F32, kind="ExternalOutput")
        self.d_xout = D("xout", [128, KC, T], F32, kind="ExternalOutput")
        self.d_mlb = D("mlb", [128, 4], F32, kind="ExternalInput", track=False)
        if self.dbg:
            self.d_dbg = D("dbg", [3, 128, 8, T], BF16, kind="ExternalOutput", blk=128 * 8 * T)

        self.xT = S.sb("xT_s", [128, KC, T], F32, blk=512)
        self.hT = S.sb("hT_s", [128, KC, T], BF16, blk=512)
        self.hh = S.sb("hh_s", [128, KC, 4], BF16)
        self.xh = S.sb("xh_s", [128, KC, 4], F32)
        self.oT = S.sb("oT_s", [128, 8, T], BF16, blk=512)
        self.NBUF = 2
        self.wbuf = [S.sb("wbuf%d" % i, [128, KC * G], BF16) for i in range(self.NBUF)]
        self.wslot = 0
        self.pp = S.sb("pp_s", [128, NP], F32)
        self.rb = S.sb("rb_s", [128, NR], F32)
        self.sel = S.sb("sel_s", [128, 8], F32)
        self.mlb = S.sb("mlb_s", [128, 4], F32)
        self.ident_bf = S.sb("ident_bf", [128, 128], BF16)
        self.ones_bf = S.sb("ones_bf", [128, 128], BF16)
        self.lb = S.sb("lb_s", [128, 4, 8], F32)
        self.omlb = S.sb("omlb_s", [128, 4, 8], F32)
        self.c8 = S.sb("c8_s", [128, 4, 8], F32)
        self.c16 = S.sb("c16_s", [128, 4, 8], F32)
        self.tiny = S.sb("tiny_s", [128, 256], F32, blk=32)
        self.NBLK = 62
        self.arena = S.sb("arena", [128, self.NBLK * 256], F32, blk=256)
        self.scr = [self.af(0, 5)[:, 0:1028]] + [self.af(5 + 4 * i, 4) for i in range(8)]
        self.bscr = [self.ab(37 + 2 * i, 2) for i in range(2)]
        self.oca = self.ab(41, 16).rearrange("p (h t) -> p h t", h=8)
        self.wab = self.ab(57, 4).rearrange("p (a h j) -> p a h j", a=2, h=8)
        self.cst = self.af(0, 2)
        self.cmask = S.sb("cmask_s", [128, T], U8)
        self.maskA = S.sb("maskA_s", [128, 128], U8)
        self.tri_bf = S.sb("tri_bf", [128, 128], BF16)
        self.ntri_bf = S.sb("ntri_bf", [128, 128], BF16)
        self.negrep = S.sb("negrep_s", [128, 4, 128], BF16)
        self.scTm = [S.sb("scTm%d" % i, [128, 128], BF16) for i in range(2)]
        self.Sst = S.sb("Sst_s", [128, 128], F32)
        self.Sbf = [S.sb("Sbf%d" % i, [128, 128], BF16) for i in range(2)]
        self.small = S.sb("small_s", [128, 1792], F32, blk=128)
        self.wdt_s = self.ab(61, 1)[:, 0:KC * 16]
        self.psum = S.ps("psum", [128, 4096], F32, blk=512)
        self.pbank = 0
        self.cc_id = 0

        ident_f = self.cst[:, C_ID:C_ID + 128]
        S.dma("sp", "ld_x", self.xT[:], self.d_xT)
        S.dma("sp", "ld_xh", self.xh[:], self.d_xh)
        S.dma("sp", "ld_pp", self.pp[:], self.d_pp)
        S.dma("sp", "ld_rb", self.rb[:], self.d_rb)
        S.dma("sp", "ld_sel", self.sel[:], self.d_sel)
        S.dma("sp", "ld_mlb", self.mlb[:], self.d_mlb)
        S.dma("sp", "ld_cst", self.cst[:], self.d_cst)
        S.copy(self.ident_bf[:], ident_f)
        S.memset(self.ones_bf[:], 1.0)
        S.memset(self.cmask[:], 1.0)
        S.memset(self.cmask[:].rearrange("p (c k) -> p c k", k=64)[:, :, 0:1], 0.0)
        S.copy(self.maskA[:], self.cst[:, C_MA:C_MA + 128])
        S.copy(self.tri_bf[:], self.cst[:, C_TRI:C_TRI + 128])
        S.ts(self.ntri_bf[:], self.cst[:, C_TRI:C_TRI + 128], -1.0, None, ALU.mult)
        S.copy(self.negrep[:], self.cst[:, C_NEG:C_NEG + 128].unsqueeze(1).to_broadcast([128, 4, 128]))
        for i in range(2):
            S.memset(self.scTm[i][:], 0.0)
        self.setup_params()
        for l in range(L):
            self.layer(l)
        self.final()
        S.wait_all("sp")
        S.emit()

    def setup_params(self):
        S = self.S
        tv = self.tiny
        lg = self.pp[:, PP_LBL:PP_LBL + 32].rearrange("p (l h) -> p l h", l=4)
        m = tv[:, 0:8]
        S.tt(m, lg[:, 0, :], lg[:, 1, :], ALU.max)
        S.tt(m, m, lg[:, 2, :], ALU.max)
        S.tt(m, m, lg[:, 3, :], ALU.max)
        e = tv[:, 32:64].rearrange("p (l h) -> p l h", l=4)
        S.tt(e, lg, m.unsqueeze(1).to_broadcast([128, 4, 8]), ALU.subtract)
        S.act(e, e, AF.Exp)
        ss = tv[:, 8:16]
        S.tt(ss, e[:, 0, :], e[:, 1, :], ALU.add)
        S.tt(ss, ss, e[:, 2, :], ALU.add)
        S.tt(ss, ss, e[:, 3, :], ALU.add)
        S.recip(ss, ss)
        S.tt(e, e, ss.unsqueeze(1).to_broadcast([128, 4, 8]), ALU.mult)
        S.memset(self.lb[:, 0, :], 0.0)
        S.copy(self.lb[:, 1, :], e[:, 1, :])
        S.tt(self.lb[:, 2, :], e[:, 1, :], e[:, 2, :], ALU.add)
        S.tt(self.lb[:, 3, :], self.lb[:, 2, :], e[:, 3, :], ALU.add)
        if self.generic:
            S.ts(self.lb[:, 0, :], e[:, 0, :], self.mlb[:, 0:1], None, ALU.mult)
            for i in range(1, 4):
                S.stt(self.lb[:, 0, :], e[:, i, :], self.mlb[:, i:i + 1], self.lb[:, 0, :], ALU.mult, ALU.add)
        S.ts(self.omlb[:], self.lb[:], -1.0, 1.0, ALU.mult, ALU.add)
        for l in range(self.L):
            lam = self.pp[:, PP_L0 + l * PL + O_LAM:PP_L0 + l * PL + O_LAM + 8]
            t = tv[:, 64:72]
            S.act(t, lam, AF.Exp, scale=-1.0)
            S.act(t, t, AF.Ln, bias=1.0)
            S.ts(self.c8[:, l, :], t, -8.0, None, ALU.mult)
            S.ts(self.c16[:, l, :], t, -16.0, None, ALU.mult)

    def dump(self, name, ap):
        if not self.dbg:
            return
        if not hasattr(self, "dumps"):
            self.dumps = {}
        if name in self.dumps:
            return
        d = self.S.dram("dump_" + name, list(ap.shape), ap.dtype, kind="ExternalOutput")
        self.dumps[name] = d
        self.S.dma("sp", "dump", d, ap)

    def af(self, b0, nb):
        return self.arena[:, b0 * 256:(b0 + nb) * 256]

    def ab(self, b0, nb):
        return self.arena[:, b0 * 256:(b0 + nb) * 256].bitcast(BF16)

    def P(self, l, off, n=1):
        b = PP_L0 + l * PL + off
        return self.pp[:, b:b + n]

    def bank(self, n=1):
        b = self.pbank
        if b + n > 6:
            b = 0
        self.pbank = (b + n) % 6
        return self.psum[:, b * 512:(b + n) * 512]

    def load_w(self, src, ncols_total):
        slot = self.wslot
        self.wslot = (self.wslot + 1) % self.NBUF
        dst = self.wbuf[slot][:, 0:ncols_total]
        self.S.dma("pool", "w%d" % slot, dst, src, max_dma_last_dim=8192)
        return self.wbuf[slot]

    def wmain(self, l, gi):
        w = self.load_w(self.d_wmain[l, gi], KC * G)
        return w[:, :].rearrange("p (k c) -> p k c", k=KC)

    def wgate(self, l, gi):
        w = self.load_w(self.d_wgate[l, gi], KC * G)
        return w[:, :].rearrange("p (k c) -> p k c", k=KC)

    def wout(self, l, n, g):
        w = self.load_w(self.d_wout[l, n, g], 8 * G)
        return w[:, 0:8 * G].rearrange("p (k c) -> p k c", k=8)

    def proj_fm(self, w, j, halo_ps=None):
        S = self.S
        ps = self.bank(2)
        hT, hh = self.hT, self.hh

        def fn(e):
            ins = None
            for half in range(2):
                for kc in range(KC):
                    ins = e.matmul(ps[:, half * 512:(half + 1) * 512], w[:, kc, j * 128:(j + 1) * 128],
                                   hT[:, kc, half * 512:(half + 1) * 512], start=(kc == 0), stop=(kc == KC - 1))
            if halo_ps is not None:
                for kc in range(KC):
                    ins = e.matmul(halo_ps, w[:, kc, j * 128:(j + 1) * 128], hh[:, kc, 0:4],
                                   start=(kc == 0), stop=(kc == KC - 1))
            return ins
        outs = [ps] + ([halo_ps] if halo_ps is not None else [])
        S.add("pe", fn, outs, [w[:, :, j * 128:(j + 1) * 128], hT[:], hh[:]])
        return ps

    def layer(self, l):
        S = self.S
        self.rmsnorm_in(l)
        if l > 0 or True:
            pass
        S.dma("pool", "ld_wab", self.ab(57, 4), self.d_wab[l], max_dma_last_dim=8192)
        for br in self.branches:
            if br == "B":
                self.branch_B(l)
                self.out_proj(l, 1)
            elif br == "A":
                self.branch_A(l)
                self.out_proj(l, 0)
            elif br == "C":
                self.branch_C(l)
                self.out_proj(l, 2)
        if l + 1 < self.L:
            self.halo_exchange(l)

    def rmsnorm_in(self, l):
        S = self.S
        xT, hT = self.xT, self.hT
        for half in range(2):
            sl = slice(half * 512, (half + 1) * 512)
            ps = self.bank(1)
            sqs = []
            for kc in range(KC):
                sq = self.bscr[kc % 2][:, 0:512]
                S.act(sq, xT[:, kc, sl], AF.Square)
                S.mm(ps, [(self.ones_bf[:], sq)], start=(kc == 0), stop=(kc == KC - 1))
            rstd = self.scr[0][:, 0:512]
            S.act(rstd, ps, AF.Sqrt, bias=EPS, scale=1.0 / 2048.0)
            S.recip(rstd, rstd)
            for kc in range(KC):
                S.stt(hT[:, kc, sl], xT[:, kc, sl], self.P(l, O_NW + kc), rstd, ALU.mult, ALU.mult)
        xh = self.xh
        sqh = self.bscr[0][:, 0:64].rearrange("p (k c) -> p k c", k=KC)
        S.act(sqh, xh[:], AF.Square)
        psh = self.psum[:, 7 * 512:7 * 512 + 4]
        S.mm(psh, [(self.ones_bf[:], sqh[:, kc, :]) for kc in range(KC)])
        rh = self.tiny[:, 96:100]
        S.act(rh, psh, AF.Sqrt, bias=EPS, scale=1.0 / 2048.0)
        S.recip(rh, rh)
        th = self.scr[0][:, 512:576].rearrange("p (k c) -> p k c", k=KC)
        S.tt(th, xh[:], self.P(l, O_NW, 16).unsqueeze(2).to_broadcast([128, KC, 4]), ALU.mult)
        S.tt(self.hh[:], th, rh.unsqueeze(1).to_broadcast([128, KC, 4]), ALU.mult)

    def exchange(self, src_sb, width, name):
        S = self.S
        i = self.cc_id
        self.cc_id += 1
        if not hasattr(self, "ccbufs"):
            self.ccbufs = {}
        if name not in self.ccbufs:
            self.ccbufs[name] = (S.dram("ccin_" + name, [128, width], F32),
                                 S.dram("ccout_" + name, [4 * 128, width], F32))
        bin_, bout = self.ccbufs[name]
        if isinstance(src_sb, list):
            for (src, off, w) in src_sb:
                S.dma("sp", "cc_st_" + name, bin_[:, off:off + w], src)
        else:
            S.dma("sp", "cc_st_" + name, bin_, src_sb)
        S.coll("cc%d" % i, "AllGather", [[0, 1, 2, 3], [4, 5, 6, 7]], bin_, bout)
        return bout

    def halo_exchange(self, l):
        S = self.S
        src = self.scr[1][:, 0:64]
        S.copy(src.rearrange("p (k c) -> p k c", k=KC), self.xT[:, :, T - 4:T])
        bout = self.exchange(src, 64, "halo")
        g = self.scr[2][:, 0:256]
        S.dma("sp", "cc_ld_halo", g.rearrange("p (r c) -> p r c", r=4), bout.rearrange("(r p) c -> p r c", p=128))
        gv = g.rearrange("p (r k c) -> p r k c", r=4, k=KC)
        xh = self.xh
        S.ts(xh[:, :, 0:3], gv[:, 0, :, 1:4], self.sel[:, 4:5], None, ALU.mult)
        for m in range(1, 4):
            S.stt(xh[:, :, 0:3], gv[:, m, :, 1:4], self.sel[:, 4 + m:5 + m], xh[:, :, 0:3], ALU.mult, ALU.add)

    def branch_B(self, l):
        S = self.S
        xb_, xc_, r_, ig_, a2_, hl_, ca_, sz_ = [self.scr[i] for i in range(8)]
        zero = self.scr[8][:, 0:T]
        S.memset(zero, 0.0)
        hlast = self.tiny[:, 128:136]
        atot = self.tiny[:, 136:144]
        wx_list = {}
        for h in range(8):
            gi, j = (COL_BX // G) + h // 2, h % 2
            if j == 0:
                wcur = self.wmain(l, gi)
            halo_ps = self.psum[:, 6 * 512:6 * 512 + 4]
            ps = self.proj_fm(wcur, j, halo_ps)
            xb = xb_[:, 0:T + 3]
            S.copy(xb[:, 3:T + 3], ps, eng="act")
            S.copy(xb[:, 0:3], halo_ps[:, 0:3], eng="act")
            xc = xc_[:, 0:T]
            S.ts(xc, xb[:, 0:T], self.P(l, O_RCW + 0 * 8 + h), self.P(l, O_RCB + h), ALU.mult, ALU.add)
            for k in range(1, 4):
                S.stt(xc, xb[:, k:k + T], self.P(l, O_RCW + k * 8 + h), xc, ALU.mult, ALU.add)
            xcb = self.bscr[0][:, 0:T]
            S.copy(xcb, xc, eng="act")
            if h == 0 and l == 0:
                self.dump("hT0", self.hT[:, 0, :]); self.dump("xb", xb); self.dump("xc", xc); self.dump("hh", self.hh[:])
            r = r_[:, 0:T]
            ig = ig_[:, 0:T]
            for (dst, which, boff) in ((r, 0, O_BA), (ig, 1, O_BX)):
                psg = self.bank(2)
                for half in range(2):
                    S.mm(psg[:, half * 512:(half + 1) * 512], [(self.wab[:, which, h, :], xcb[:, half * 512:(half + 1) * 512])])
                S.act(dst, psg, AF.Sigmoid, bias=self.P(l, boff + h))
            a2 = a2_[:, 0:T]
            S.act(a2, r, AF.Exp, scale=self.c16[:, l, h:h + 1])
            S.act(r, r, AF.Exp, scale=self.c8[:, l, h:h + 1])
            S.ts(a2, a2, -1.0, 1.0, ALU.mult, ALU.add)
            S.act(a2, a2, AF.Sqrt)
            S.tt(ig, ig, xc, ALU.mult)
            S.tt(a2, a2, ig, ALU.mult)
            hl = hl_[:, 0:T]
            ca = ca_[:, 0:T]
            if h == 0 and l == 0:
                self.dump("a", r); self.dump("u", a2); self.dump("igx", ig)
            S.scan(hl, r, a2, 0.0)
            S.scan(ca, r, zero, 1.0)
            S.copy(hlast[:, h:h + 1], hl[:, T - 1:T])
            S.copy(atot[:, h:h + 1], ca[:, T - 1:T])
            gi, j = (COL_BZ // G) + h // 2, h % 2
            if j == 0:
                wz = self.wmain(l, gi)
            psz = self.proj_fm(wz, j)
            sz = sz_[:, 0:T]
            S.act(sz, psz, AF.Silu)
            if h == 0 and l == 0:
                self.dump("hl", hl); self.dump("ca", ca); self.dump("sz", sz)
            S.tt(self.oT[:, h, :], hl, sz, ALU.mult)
            S.tt(self.oca[:, h, :], ca, sz, ALU.mult)
        src = self.tiny[:, 128:144]
        bout = self.exchange(src, 16, "B")
        g = self.tiny[:, 160:224]
        S.dma("sp", "cc_ld_B", g.rearrange("p (r c) -> p r c", r=4), bout.rearrange("(r p) c -> p r c", p=128))
        gv = g.rearrange("p (r c) -> p r c", r=4)
        hin = self.tiny[:, 144:152]
        self.horner(hin, lambda m: gv[:, m, 8:16], lambda m: gv[:, m, 0:8], 8)
        for h in range(8):
            S.stt(self.oT[:, h, :], self.oca[:, h, :], hin[:, h:h + 1], self.oT[:, h, :], ALU.mult, ALU.add)

    def horner(self, tacc, dec, val, width):
        S = self.S
        tmp = self.tiny[:, 224:224 + width]
        tmp2 = self.tiny[:, 240:240 + width]
        S.memset(tacc, 0.0)
        for m in range(3):
            pm = self.sel[:, m:m + 1]
            S.ts(tmp, dec(m), -1.0, pm, ALU.add, ALU.mult)
            S.ts(tmp, tmp, 1.0, None, ALU.add)
            S.tt(tacc, tacc, tmp, ALU.mult)
            S.ts(tmp2, val(m), pm, None, ALU.mult)
            S.tt(tacc, tacc, tmp2, ALU.add)

    def out_proj(self, l, n):
        S = self.S
        if self.dbg and l == 0:
            S.dma("sp", "dbg", self.d_dbg[n], self.oT[:])
        for g in range(8):
            wg = self.wgate(l, n * 8 + g)
            wo = self.wout(l, n, g)
            for j in range(2):
                dc = g * 2 + j
                for half in range(2):
                    sl = slice(half * 512, (half + 1) * 512)
                    psg = self.bank(1)
                    S.mm(psg, [(wg[:, kc, j * 128:(j + 1) * 128], self.hT[:, kc, sl]) for kc in range(KC)])
                    gate = self.scr[half][:, 0:512]
                    S.act(gate, psg, AF.Sigmoid)
                    pso = self.bank(1)
                    S.mm(pso, [(wo[:, cc, j * 128:(j + 1) * 128], self.oT[:, cc, sl]) for cc in range(8)])
                    S.tt(gate, pso, gate, ALU.mult)
                    S.tt(self.xT[:, dc, sl], self.xT[:, dc, sl], gate, ALU.add, eng="pool")

    def final(self):
        S = self.S
        xT = self.xT
        for kc in range(KC):
            S.dma("sp", "st_xout", self.d_xout[:, kc, :], xT[:, kc, :])
        for half in range(2):
            sl = slice(half * 512, (half + 1) * 512)
            ps = self.bank(1)
            for kc in range(KC):
                sq = self.bscr[kc % 2][:, 0:512]
                S.act(sq, xT[:, kc, sl], AF.Square)
                S.mm(ps, [(self.ones_bf[:], sq)], start=(kc == 0), stop=(kc == KC - 1))
            rstd = self.scr[0][:, 0:512]
            S.act(rstd, ps, AF.Sqrt, bias=EPS, scale=1.0 / 2048.0)
            S.recip(rstd, rstd)
            for kc in range(KC):
                S.stt(xT[:, kc, sl], xT[:, kc, sl], self.pp[:, PP_FNW + kc:PP_FNW + kc + 1], rstd, ALU.mult, ALU.mult)
        for kc in range(KC):
            S.dma("sp", "st_out", self.d_out[:, kc, :], xT[:, kc, :])

    def proj_tm(self, w, ncols, tc0, ps):
        hT = self.hT

        def fn(e):
            ins = None
            for i in range(2):
                tc = tc0 + i
                for kc in range(KC):
                    ins = e.matmul(ps[:, i * ncols:(i + 1) * ncols], hT[:, kc, tc * 128:(tc + 1) * 128],
                                   w[:, kc, 0:ncols], start=(kc == 0), stop=(kc == KC - 1))
            return ins
        self.S.add("pe", fn, [ps[:, 0:2 * ncols]], [w[:, :, 0:ncols], hT[:]])

    def branch_A(self, l):
        S = self.S
        vtok = self.ab(0, 16).rearrange("p (c v) -> p c v", c=8)
        qc = self.ab(16, 16).rearrange("p (h t) -> p h t", h=8)
        Lt, KK, Bc, BM, E2 = [self.af(32 + 4 * i, 4) for i in range(5)]
        qg, kg, kgT = [self.ab(52 + 2 * i, 2) for i in range(3)]
        stA = self.af(58, 4)
        dtotA = self.small[:, 416:424]
        sm = self.small
        for g in range(4):
            w = self.wmain(l, COL_AI // G + g)
            for tc0 in range(0, 8, 2):
                ps = self.bank(1)
                self.proj_tm(w, G, tc0, ps)
                S.copy(vtok[:, tc0:tc0 + 2, g * G:(g + 1) * G], ps.rearrange("p (i c) -> p i c", i=2), eng="act")
        c3 = lambda ap: ap.rearrange("p (c k) -> p c k", k=64)
        for h in range(8):
            j = h % 2
            if j == 0:
                wq = self.wmain(l, COL_AQ // G + h // 2)
                wf = self.wmain(l, COL_AF // G + h // 2)
            psf = self.proj_fm(wf, j)
            S.act(Lt, psf, AF.Sigmoid)
            S.act(KK, psf, AF.Sigmoid, scale=-1.0)
            psq = self.proj_fm(wq, j)
            S.ts(Lt, Lt, self.omlb[:, l, h:h + 1], self.lb[:, l, h:h + 1], ALU.mult, ALU.add)
            S.act(Lt, Lt, AF.Ln)
            S.scan(Bc, self.cmask[:], Lt, 0.0)
            S.tt(c3(BM), c3(Bc), c3(Bc)[:, :, 31:32].to_broadcast([128, 16, 64]), ALU.subtract)
            eR, eL, eT, eP, blast, incl, eRP = [sm[:, 16 * i:16 * i + 16] for i in range(7)]
            S.act(eR, c3(Bc)[:, :, 31], AF.Exp)
            S.act(eL, c3(BM)[:, :, 63], AF.Exp)
            S.act(eT, c3(Bc)[:, :, 63], AF.Exp)
            S.copy(blast, c3(Bc)[:, :, 63])
            S.scan(incl, sm[:, 128:144], blast, 0.0) if False else None
            S.memset(sm[:, 128:144], 1.0)
            S.scan(incl, sm[:, 128:144], blast, 0.0)
            S.act(dtotA[:, h:h + 1], incl[:, 15:16], AF.Exp)
            S.tt(eP, incl, blast, ALU.subtract)
            S.act(eP, eP, AF.Exp)
            S.tt(eRP, eR, eP, ALU.mult)
            S.act(E2, BM, AF.Exp, scale=-1.0)
            S.act(BM, BM, AF.Exp)
            S.tt(qg, psq, BM, ALU.mult)
            S.stt(kg, KK, self.omlb[:, l, h:h + 1], E2, ALU.mult, ALU.mult)
            S.tt(c3(qc[:, h, :]), c3(qg), eRP.unsqueeze(2).to_broadcast([128, 16, 64]), ALU.mult)
            pst = self.bank(1).bitcast(BF16)
            for tc in range(8):
                S.transpose(pst[:, tc * 128:(tc + 1) * 128], kg[:, tc * 128:(tc + 1) * 128], self.ident_bf[:])
            S.copy(kgT, pst, eng="act")
            Sst = self.Sst
            for tc in range(8):
                tsl = slice(tc * 128, (tc + 1) * 128)
                psS = self.bank(1)[:, 0:128]
                S.mm(psS, [(kg[:, tsl], qg[:, tsl])])
                sc = self.scTm[tc % 2]
                S.add("dve", lambda e, sc=sc, psS=psS: e.copy_predicated(out=sc[:], mask=self.maskA[:], data=psS),
                      [sc[:]], [self.maskA[:], psS])
                pso = self.bank(1)[:, 0:128]
                S.mm(pso, [(vtok[:, tc, h * 128:(h + 1) * 128], sc[:])], start=True, stop=False)
                for i in range(2):
                    c = 2 * tc + i
                    rows = slice(i * 64, (i + 1) * 64)
                    if c > 0:
                        sb_ = self.Sbf[c % 2]
                        S.ts(sb_[:], Sst[:], eR[:, c:c + 1], None, ALU.mult)
                        S.mm(pso[:, i * 64:(i + 1) * 64], [(sb_[:], qg[:, c * 64:(c + 1) * 64])], start=False, stop=(i == 1))
                    elif True:
                        pass
                    psU = self.bank(1)[:, 0:128]
                    S.mm(psU, [(kgT[rows, tc * 128:(tc + 1) * 128], vtok[rows, tc, h * 128:(h + 1) * 128])])
                    if c == 0:
                        S.ts(Sst[:], psU, eL[:, c:c + 1], None, ALU.mult)
                    else:
                        tmpU = sm[:, 256:384]
                        S.ts(tmpU, psU, eL[:, c:c + 1], None, ALU.mult)
                        S.stt(Sst[:], Sst[:], eT[:, c:c + 1], tmpU, ALU.mult, ALU.add)
                S.copy(self.oT[:, h, tsl], pso, eng="act")
            S.copy(stA[:, h * 128:(h + 1) * 128], Sst[:])
        bout = self.exchange([(stA, 0, 1024), (dtotA, 1024, 8)], 1032, "A")
        gth = self.af(32, 13)[:, 0:3 * 1032].rearrange("p (r c) -> p r c", r=3)
        S.dma("sp", "cc_ld_A", gth, bout.rearrange("(r p) c -> p r c", p=128)[:, 0:3, :])
        Tin = self.af(45, 4)
        S.memset(Tin, 0.0)
        dpr = sm[:, 400:408]
        for m in range(3):
            pm = self.sel[:, m:m + 1]
            S.ts(dpr, gth[:, m, 1024:1032], -1.0, pm, ALU.add, ALU.mult)
            S.ts(dpr, dpr, 1.0, None, ALU.add)
            S.tt(Tin.rearrange("p (h v) -> p h v", h=8), Tin.rearrange("p (h v) -> p h v", h=8),
                 dpr.unsqueeze(2).to_broadcast([128, 8, 128]), ALU.mult)
            S.stt(Tin, gth[:, m, 0:1024], pm, Tin, ALU.mult, ALU.add)
        Sin = self.ab(49, 2)
        S.copy(Sin, Tin, eng="act")
        O_, sz_, rs_ = self.af(53, 4), self.af(57, 4), self.af(32, 4)
        sqb = self.ab(51, 2)
        for h in range(8):
            j = h % 2
            if j == 0:
                wz = self.wmain(l, COL_AZ // G + h // 2)
            psz = self.proj_fm(wz, j)
            S.act(sz_, psz, AF.Silu)
            psc = self.bank(2)
            for half in range(2):
                sl = slice(half * 512, (half + 1) * 512)
                S.mm(psc[:, sl], [(Sin[:, h * 128:(h + 1) * 128], qc[:, h, sl])])
            S.tt(O_, psc, self.oT[:, h, :], ALU.add)
            S.act(sqb, O_, AF.Square)
            psn = self.bank(2)
            for half in range(2):
                sl = slice(half * 512, (half + 1) * 512)
                S.mm(psn[:, sl], [(self.ones_bf[:], sqb[:, sl])])
            S.act(rs_, psn, AF.Sqrt, bias=EPS, scale=1.0 / 128.0)
            S.recip(rs_, rs_)
            S.stt(O_, O_, self.P(l, O_HNW + h), rs_, ALU.mult, ALU.mult)
            S.tt(self.oT[:, h, :], O_, sz_, ALU.mult)

    def branch_C(self, l):
        S = self.S
        sm = self.small
        xs_tok = self.ab(0, 16).rearrange("p (c v) -> p c v", c=8)
        szt = self.ab(16, 16).rearrange("p (c v) -> p c v", c=8)
        BT = self.ab(32, 4).rearrange("p (g t) -> p g t", g=2)
        CT = self.ab(36, 4).rearrange("p (g t) -> p g t", g=2)
        Btok = self.ab(40, 4).rearrange("p (c g n) -> p c g n", c=8, g=2)
        xb = self.af(50, 5)[:, 0:T + 3]
        xc = self.af(55, 4)
        xsb = self.ab(59, 2)
        S.dma("pool", "ld_wdt", self.wdt_s, self.d_wdt[l], max_dma_last_dim=8192)
        wdt = self.wdt_s.rearrange("p (k c) -> p k c", k=KC)
        psdt = self.bank(1)[:, 0:128]
        hT = self.hT

        def fn(e):
            ins = None
            for tc in range(8):
                for kc in range(KC):
                    ins = e.matmul(psdt[:, tc * 16:(tc + 1) * 16], hT[:, kc, tc * 128:(tc + 1) * 128], wdt[:, kc, :],
                                   start=(kc == 0), stop=(kc == KC - 1))
            return ins
        S.add("pe", fn, [psdt], [wdt, hT[:]])
        rbl = self.rb[:, l * RB_L:(l + 1) * RB_L]
        v3 = lambda ap: ap.rearrange("p (c h) -> p c h", c=8)
        dt, da, dahi, cum, tot, ecum, decs, edec = [sm[:, 512 + 128 * i:640 + 128 * i] for i in range(8)]
        arep = sm[:, 448:464]
        S.tt(v3(dt), v3(psdt), rbl[:, 0:16].unsqueeze(1).to_broadcast([128, 8, 16]), ALU.add)
        S.act(dt, dt, AF.Exp)
        S.act(dt, dt, AF.Ln, bias=1.0)
        S.act(arep, rbl[:, 16:32], AF.Exp)
        S.ts(arep, arep, -1.0, None, ALU.mult)
        S.tt(v3(da), v3(dt), arep.unsqueeze(1).to_broadcast([128, 8, 16]), ALU.mult)
        dah_b = sm[:, 1536:1664].bitcast(BF16)[:, 0:128]
        dal_b = sm[:, 1536:1664].bitcast(BF16)[:, 128:256]
        S.copy(dah_b, da)
        S.tt(dahi, da, dah_b, ALU.subtract)
        S.copy(dal_b, dahi)
        pcum = self.bank(1)
        S.mm(pcum[:, 0:128], [(self.tri_bf[:], dah_b), (self.tri_bf[:], dal_b)])
        S.mm(pcum[:, 128:256], [(self.ones_bf[:], dah_b), (self.ones_bf[:], dal_b)])
        S.copy(cum, pcum[:, 0:128], eng="act")
        S.copy(tot, pcum[:, 128:256], eng="act")
        S.act(ecum, cum, AF.Exp)
        S.tt(decs, tot, cum, ALU.subtract)
        S.act(decs, decs, AF.Exp)
        S.act(edec, tot, AF.Exp)
        for g in range(4):
            w = self.wmain(l, COL_CZ // G + g)
            for tc0 in range(0, 8, 2):
                ps = self.bank(1)
                self.proj_tm(w, G, tc0, ps)
                S.act(szt[:, tc0:tc0 + 2, g * G:(g + 1) * G], ps.rearrange("p (i c) -> p i c", i=2), AF.Silu)
        for c in range(12):
            j = c % 2
            if j == 0:
                wx = self.wmain(l, COL_CX // G + c // 2)
            halo_ps = self.psum[:, 6 * 512:6 * 512 + 4]
            ps = self.proj_fm(wx, j, halo_ps)
            S.copy(xb[:, 3:T + 3], ps, eng="act")
            S.copy(xb[:, 0:3], halo_ps[:, 0:3], eng="act")
            S.ts(xc, xb[:, 0:T], self.P(l, O_SCW + 0 * 12 + c), self.P(l, O_SCB + c), ALU.mult, ALU.add)
            for k in range(1, 4):
                S.stt(xc, xb[:, k:k + T], self.P(l, O_SCW + k * 12 + c), xc, ALU.mult, ALU.add)
            if c < 8:
                dst = xsb
            elif c < 10:
                dst = BT[:, c - 8, :]
            else:
                dst = CT[:, c - 10, :]
            S.act(dst, xc, AF.Silu)
            if c < 10:
                pst = self.bank(1).bitcast(BF16)
                for tc in range(8):
                    S.transpose(pst[:, tc * 128:(tc + 1) * 128], dst[:, tc * 128:(tc + 1) * 128], self.ident_bf[:])
                pv = pst.rearrange("p (c v) -> p c v", c=8)
                if c < 8:
                    S.copy(xs_tok[:, :, c * 128:(c + 1) * 128], pv, eng="act")
                else:
                    S.copy(Btok[:, :, c - 8, :], pv, eng="act")
        stT = self.af(44, 4)
        stb = self.ab(48, 2)
        eg = self.ab(50, 4)
        dab = self.ab(54, 2)
        xdt = self.ab(56, 2)
        y = self.af(58, 4)
        cbT = sm[:, 1664:1792].bitcast(BF16)
        h3 = lambda ap: ap.rearrange("p (h q) -> p h q", h=16)

        def state_step(tc, first):
            S.tt(h3(xdt), h3(xdt), decs[:, tc * 16:(tc + 1) * 16].unsqueeze(2).to_broadcast([128, 16, 64]), ALU.mult)
            psn = self.bank(2)
            for g in range(2):
                S.mm(psn[:, g * 512:(g + 1) * 512], [(Btok[:, tc, g, :], xdt[:, g * 512:(g + 1) * 512])])
            if not first:
                S.tt(h3(stT), h3(stT), edec[:, tc * 16:(tc + 1) * 16].unsqueeze(2).to_broadcast([128, 16, 64]), ALU.mult)
                S.tt(stT, stT, psn, ALU.add)
            else:
                S.copy(stT, psn)

        def make_xdt(tc):
            S.tt(h3(xdt), h3(xs_tok[:, tc, :]), dt[:, tc * 16:(tc + 1) * 16].unsqueeze(2).to_broadcast([128, 16, 64]), ALU.mult)

        for tc in range(8):
            make_xdt(tc)
            state_step(tc, tc == 0)
        dtotC = sm[:, 424:440]
        tsum = sm[:, 464:480]
        S.tt(tsum, tot[:, 0:16], tot[:, 16:32], ALU.add)
        for tc in range(2, 8):
            S.tt(tsum, tsum, tot[:, tc * 16:(tc + 1) * 16], ALU.add)
        S.act(dtotC, tsum, AF.Exp)
        bout = self.exchange([(stT, 0, 1024), (dtotC, 1024, 16)], 1040, "C")
        gth = self.af(49, 13)[:, 0:3 * 1040].rearrange("p (r c) -> p r c", r=3)
        S.dma("sp", "cc_ld_C", gth, bout.rearrange("(r p) c -> p r c", p=128)[:, 0:3, :])
        S.memset(stT, 0.0)
        dpr = sm[:, 480:496]
        for m in range(3):
            pm = self.sel[:, m:m + 1]
            S.ts(dpr, gth[:, m, 1024:1040], -1.0, pm, ALU.add, ALU.mult)
            S.ts(dpr, dpr, 1.0, None, ALU.add)
            S.tt(h3(stT), h3(stT), dpr.unsqueeze(2).to_broadcast([128, 16, 64]), ALU.mult)
            S.stt(stT, gth[:, m, 0:1024], pm, stT, ALU.mult, ALU.add)
        drep = rbl[:, 32:48]
        for tc in range(8):
            tsl = slice(tc * 128, (tc + 1) * 128)
            S.copy(stb, stT, eng="act")
            dav = lambda t: t[:, tc * 16:(tc + 1) * 16]
            dab3 = dab.rearrange("p (a h l) -> p a h l", a=2, h=4)
            seg = self.psum[:, 2048:4096]
            for q in range(4):
                hs = slice(q * 4, (q + 1) * 4)
                S.copy(dab3[:, 0, :, :], dav(dah_b)[:, hs].unsqueeze(2).to_broadcast([128, 4, 128]))
                S.copy(dab3[:, 1, :, :], dav(dal_b)[:, hs].unsqueeze(2).to_broadcast([128, 4, 128]))
                bank = seg[:, q * 512:(q + 1) * 512]

                def fn(e, bank=bank):
                    e.matmul(bank, self.ntri_bf[:], dab3[:, 0, :, :], start=True, stop=False)
                    e.matmul(bank, self.ntri_bf[:], dab3[:, 1, :, :], start=False, stop=False)
                    ins = e.matmul(bank, self.ident_bf[:], self.negrep[:, :, :], start=False, stop=False)
                    for hh in range(4):
                        for a in range(2):
                            ins = e.matmul(bank[:, hh * 128:(hh + 1) * 128], dab3[:, a, hh, :], self.tri_bf[:],
                                           start=False, stop=(hh == 3 and a == 1))
                    return ins
                S.add("pe", fn, [bank], [dab, self.ntri_bf[:], self.tri_bf[:], self.negrep[:], self.ident_bf[:]])
                S.act(eg[:, q * 512:(q + 1) * 512], bank, AF.Exp)
            pscb = self.bank(1)[:, 0:256]
            for g in range(2):
                S.mm(pscb[:, g * 128:(g + 1) * 128], [(BT[:, g, tsl], CT[:, g, tsl])])
            S.copy(cbT, pscb, eng="act")
            eg4 = eg.rearrange("p (g h l) -> p g h l", g=2, h=8)
            for g in range(2):
                S.tt(eg4[:, g, :, :], eg4[:, g, :, :], cbT[:, g * 128:(g + 1) * 128].unsqueeze(1).to_broadcast([128, 8, 128]), ALU.mult)
            make_xdt(tc)
            psy = self.bank(2)
            eg3 = eg.rearrange("p (h l) -> p h l", h=16)

            def fny(e, psy=psy):
                ins = None
                for hh in range(16):
                    ins = e.matmul(psy[:, hh * 64:(hh + 1) * 64], eg3[:, hh, :], xdt[:, hh * 64:(hh + 1) * 64],
                                   start=True, stop=True)
                return ins
            S.add("pe", fny, [psy], [eg, xdt])
            psyo = self.bank(2)
            for g in range(2):
                S.mm(psyo[:, g * 512:(g + 1) * 512], [(CT[:, g, tsl], stb[:, g * 512:(g + 1) * 512])])
            S.tt(h3(y), h3(xs_tok[:, tc, :]), drep.unsqueeze(2).to_broadcast([128, 16, 64]), ALU.mult)
            S.tt(y, y, psy, ALU.add)
            yo = self.af(50, 4)
            S.tt(h3(yo), h3(psyo), ecum[:, tc * 16:(tc + 1) * 16].unsqueeze(2).to_broadcast([128, 16, 64]), ALU.mult)
            S.tt(y, y, yo, ALU.add)
            state_step(tc, False)
            S.tt(y, y, szt[:, tc, :], ALU.mult)
            ss = sm[:, 496:498]
            junk = self.af(50, 2)
            for g in range(2):
                S.act(junk[:, 0:512], y[:, g * 512:(g + 1) * 512], AF.Square, accum_out=ss[:, g:g + 1])
            S.act(ss, ss, AF.Sqrt, bias=EPS, scale=1.0 / 512.0)
            S.recip(ss, ss)
            ot = self.ab(52, 2)
            for g in range(2):
                S.ts(ot[:, g * 512:(g + 1) * 512], y[:, g * 512:(g + 1) * 512], ss[:, g:g + 1], None, ALU.mult)
            pst = self.bank(1).bitcast(BF16)
            for cc in range(8):
                S.transpose(pst[:, cc * 128:(cc + 1) * 128], ot[:, cc * 128:(cc + 1) * 128], self.ident_bf[:])
            S.tt(self.oT[:, :, tsl], pst.rearrange("p (c t) -> p c t", c=8),
                 self.P(l, O_SNW, 8).unsqueeze(2).to_broadcast([128, 8, 128]), ALU.mult)


def _to_fm(xs):
    return np.ascontiguousarray(xs.reshape(xs.shape[0], KC, 128).transpose(2, 1, 0))


def host_prep_layer(inp, l, x):
    f32 = np.float32
    one = {k: (np.asarray(v)[l:l + 1] if k not in ("x", "final_norm_w", "hgrn_lb_logits") else np.asarray(v))
           for k, v in inp.items()}
    pp = np.zeros((128, NP), f32)
    lbl = np.asarray(inp["hgrn_lb_logits"]).astype(f32)
    pp[:, PP_LBL:PP_LBL + 32] = lbl.reshape(4, 8, 128).transpose(2, 0, 1).reshape(128, 32)
    pp[:, PP_FNW:PP_FNW + 16] = fm(np.asarray(inp["final_norm_w"]).astype(f32), 16)
    rb = np.zeros((128, NR), f32)
    b = PP_L0
    pp[:, b + O_NW:b + O_NW + 16] = fm(one["norm_w"][0], 16)
    pp[:, b + O_HNW:b + O_HNW + 8] = fm(one["hgrn_norm_w"][0], 8)
    for k in range(4):
        pp[:, b + O_RCW + k * 8:b + O_RCW + k * 8 + 8] = fm(one["rglru_conv_w"][0, k], 8)
        pp[:, b + O_SCW + k * 12:b + O_SCW + k * 12 + 12] = fm(one["ssd_conv_w"][0, k], 12)
    pp[:, b + O_RCB:b + O_RCB + 8] = fm(one["rglru_conv_b"][0], 8)
    pp[:, b + O_BA:b + O_BA + 8] = fm(one["rglru_ba"][0].reshape(-1), 8)
    pp[:, b + O_BX:b + O_BX + 8] = fm(one["rglru_bx"][0].reshape(-1), 8)
    pp[:, b + O_LAM:b + O_LAM + 8] = fm(one["rglru_lambda"][0], 8)
    pp[:, b + O_SCB:b + O_SCB + 12] = fm(one["ssd_conv_b"][0], 12)
    pp[:, b + O_SNW:b + O_SNW + 8] = fm(one["ssd_norm_w"][0], 8)
    rb[:, 0:16] = one["ssd_dt_bias"][0][None, :]
    rb[:, 16:32] = one["ssd_a_log"][0][None, :]
    rb[:, 32:48] = one["ssd_d"][0][None, :]
    w_in = one["w_in"]
    wmain = np.ascontiguousarray(w_in[:, :, :COL_DT].reshape(1, KC, 128, NG_MAIN, G).transpose(0, 3, 2, 1, 4))
    wdt = np.ascontiguousarray(w_in[:, :, COL_DT:COL_G].reshape(1, KC, 128, 16).transpose(0, 2, 1, 3))
    wgate = np.ascontiguousarray(w_in[:, :, COL_G:].reshape(1, KC, 128, NG_GATE, G).transpose(0, 3, 2, 1, 4))
    wout = np.ascontiguousarray(one["w_out"].reshape(1, 3, 8, 128, 8, G).transpose(0, 1, 4, 3, 2, 5))
    wab = np.ascontiguousarray(np.stack([one["rglru_wa"], one["rglru_wx"]], 1).transpose(0, 3, 1, 2, 4))
    mlb = np.zeros((128, 4), f32)
    mlb[:, 1:l + 1] = 1.0
    shared = dict(pp=pp, rb=rb, cst=host_consts(), mlb=mlb,
                  wmain=wmain.reshape(1, NG_MAIN, 128, KC * G), wdt=wdt.reshape(1, 128, KC * 16),
                  wgate=wgate.reshape(1, NG_GATE, 128, KC * G), wout=wout.reshape(1, 3, 8, 128, 8 * G),
                  wab=wab.reshape(1, 128, 2 * 8 * 128))
    percore = []
    for r in range(8):
        bb, s = r // 4, r % 4
        xT = _to_fm(x[bb, s * T:(s + 1) * T, :])
        xh = np.zeros((128, KC, 4), f32)
        if s > 0:
            xh[:, :, 0:3] = x[bb, s * T - 3:s * T, :].reshape(3, KC, 128).transpose(2, 1, 0)
        sel = np.zeros((128, 8), f32)
        for m in range(4):
            sel[:, m] = 1.0 if m < s else 0.0
            sel[:, 4 + m] = 1.0 if m == s - 1 else 0.0
        percore.append(dict(xT=xT, xh=xh, sel=sel))
    return shared, percore


_PROG = {}


def kernel(**inputs):
    inp = {k: np.ascontiguousarray(np.asarray(v), dtype=np.float32) for k, v in inputs.items()}
    if "fused" not in _PROG:
        _PROG["fused"] = Prog(L=4, branches="BAC", dbg=False, generic=False)
    prog = _PROG["fused"]
    shared, percore = host_prep(inp, 4)
    shared["wmain"] = shared["wmain"].reshape(4, NG_MAIN, 128, KC * G)
    shared["wdt"] = shared["wdt"].reshape(4, 128, KC * 16)
    shared["wgate"] = shared["wgate"].reshape(4, NG_GATE, 128, KC * G)
    shared["wout"] = shared["wout"].reshape(4, 3, 8, 128, 8 * G)
    shared["wab"] = shared["wab"].reshape(4, 128, 2 * 8 * 128)
    shared["mlb"] = np.zeros((128, 4), np.float32)
    in_maps = [dict(shared, **pc) for pc in percore]
    res = run_bass_kernel_spmd(prog.nc, in_maps, core_ids=list(range(8)))
    out = np.zeros((2, 4 * T, 2048), np.float32)
    for r in range(8):
        bb, s = r // 4, r % 4
        o = np.asarray(res.results[r]["out"])
        out[bb, s * T:(s + 1) * T, :] = o.transpose(2, 1, 0).reshape(T, 2048)
    return out
```

```python
import numpy as np
import concourse.bass as bass
import concourse.mybir as mybir

F32 = mybir.dt.float32
BF16 = mybir.dt.bfloat16
AF = mybir.ActivationFunctionType
ALU = mybir.AluOpType


def is_ap(x):
    return hasattr(x, "ap") and hasattr(x, "tensor") and hasattr(x, "offset")


class Sched:
    ENGS = ["pe", "act", "dve", "pool", "sp"]

    def __init__(self, nc, stack):
        self.nc = nc
        self.stack = stack
        self.ops = {e: [] for e in self.ENGS}
        self.sem = {}
        self.cnt = {}
        self.seen = {e: {} for e in self.ENGS}
        self.state = {}
        self.tinfo = {}
        for e in self.ENGS:
            self.newsem("E:" + e)
        self.nops = 0

    def newsem(self, key):
        if key not in self.sem:
            self.sem[key] = self.stack.enter_context(self.nc.semaphore("s%d" % len(self.sem)))
            self.cnt[key] = 0
        return self.sem[key]

    def sb(self, name, shape, dtype, blk=None):
        t = self.stack.enter_context(self.nc.sbuf_tensor(name, list(shape), dtype))
        psize = int(np.prod(shape[1:]))
        self.tinfo[name] = (psize, blk or psize, mybir.dt.size(dtype))
        return t

    def ps(self, name, shape, dtype, blk=None):
        t = self.stack.enter_context(self.nc.psum_tensor(name, list(shape), dtype))
        psize = int(np.prod(shape[1:]))
        self.tinfo[name] = (psize, blk or psize, mybir.dt.size(dtype))
        return t

    def dram(self, name, shape, dtype, kind=None, blk=None, track=True):
        if kind is None:
            t = self.nc.dram_tensor(name, list(shape), dtype)
        else:
            t = self.nc.dram_tensor(name, list(shape), dtype, kind=kind)
        if track:
            self.tinfo[name] = (None, blk or int(np.prod(shape)), mybir.dt.size(dtype))
        return t.ap()

    def _blocks(self, ap):
        name = ap.name
        info = self.tinfo.get(name)
        if info is None:
            return []
        psize, blk, esz = info
        off = int(ap.offset)
        pairs = ap.ap
        vsz = mybir.dt.size(ap.dtype)
        if psize is not None:
            vps = psize * esz // vsz
            lo = off % vps
            hi = lo + sum((c - 1) * abs(s) for s, c in pairs[1:])
        else:
            lo = off
            hi = lo + sum((c - 1) * abs(s) for s, c in pairs)
        lo = lo * vsz // esz
        hi = hi * vsz // esz
        return [(name, b) for b in range(lo // blk, hi // blk + 1)]

    def _deps(self, eng, outs, ins):
        waits = {}
        me = "E:" + eng
        seen = self.seen[eng]

        def need(ev):
            if ev is None:
                return
            sid, val = ev
            if sid == me and eng == "pe":
                return
            if seen.get(sid, 0) >= val:
                return
            if waits.get(sid, 0) < val:
                waits[sid] = val

        for ap in ins:
            for b in self._blocks(ap):
                st = self.state.get(b)
                if st:
                    need(st[0])
        for ap in outs:
            for b in self._blocks(ap):
                st = self.state.get(b)
                if st:
                    need(st[0])
                    for sid, val in st[1].items():
                        need((sid, val))
        for sid, val in waits.items():
            seen[sid] = val
        return list(waits.items())

    def _record(self, ev, outs, ins):
        sid, val = ev
        for ap in ins:
            for b in self._blocks(ap):
                st = self.state.setdefault(b, [None, {}])
                if st[1].get(sid, 0) < val:
                    st[1][sid] = val
        for ap in outs:
            for b in self._blocks(ap):
                self.state[b] = [ev, {}]

    def add(self, eng, fn, outs, ins):
        outs = [a for a in outs if is_ap(a)]
        ins = [a for a in ins if is_ap(a)]
        waits = self._deps(eng, outs, ins)
        sid = "E:" + eng
        self.cnt[sid] += 1
        ev = (sid, self.cnt[sid])
        self._record(ev, outs, ins)
        self.ops[eng].append((waits, fn, sid, 1))
        self.nops += 1

    def dma(self, queue, semkey, out, in_, **kw):
        self.newsem(semkey)
        waits = self._deps(queue, [out], [in_])
        self.cnt[semkey] += 16
        ev = (semkey, self.cnt[semkey])
        self._record(ev, [out], [in_])
        self.ops[queue].append((waits, lambda e: e.dma_start(out=out, in_=in_, **kw), semkey, 16))
        self.nops += 1

    def coll(self, semkey, kind, groups, in_ap, out_ap):
        self.newsem(semkey)
        assert self.cnt[semkey] == 0
        waits = self._deps("pool", [out_ap], [in_ap])
        self.cnt[semkey] = 1
        ev = (semkey, 1)
        self._record(ev, [out_ap], [in_ap])
        self.ops["pool"].append((waits, lambda e: e.collective_compute(
            kind, ALU.bypass, replica_groups=groups, ins=[in_ap.opt()], outs=[out_ap.opt()]), semkey, None))
        self.nops += 1

    def wait_all(self, eng):
        waits = []
        for sid, val in self.cnt.items():
            if sid == "E:" + eng or val == 0:
                continue
            if self.seen[eng].get(sid, 0) < val:
                waits.append((sid, val))
                self.seen[eng][sid] = val
        self.ops[eng].append((waits, None, None, 0))

    def emit(self):
        nc = self.nc
        with nc.Block() as block:
            def mk(engname):
                def body(e):
                    for waits, fn, sid, inc in self.ops[engname]:
                        for wsid, wval in waits:
                            e.wait_ge(self.sem[wsid], wval)
                        if fn is None:
                            continue
                        ins = fn(e)
                        if inc is None:
                            ins.then_inc(self.sem[sid])
                        else:
                            ins.then_inc(self.sem[sid], inc)
                return body
            block.tensor(mk("pe"))
            block.scalar(mk("act"))
            block.vector(mk("dve"))
            block.gpsimd(mk("pool"))
            block.sync(mk("sp"))

    def act(self, out, in_, func, bias=None, scale=None, accum_out=None):
        kw = {}
        if bias is not None:
            kw["bias"] = bias
        if scale is not None:
            kw["scale"] = scale
        if accum_out is not None:
            kw["accum_out"] = accum_out
        self.add("act", lambda e: e.activation(out=out, in_=in_, func=func, **kw),
                 [out, accum_out], [in_, bias, scale])

    def tt(self, out, in0, in1, op, eng="dve"):
        self.add(eng, lambda e: e.tensor_tensor(out=out, in0=in0, in1=in1, op=op), [out], [in0, in1])

    def ts(self, out, in0, s1, s2=None, op0=ALU.mult, op1=None, eng="dve", accum_out=None):
        kw = {}
        if op1 is not None:
            kw["op1"] = op1
        if accum_out is not None:
            kw["accum_out"] = accum_out
        self.add(eng, lambda e: e.tensor_scalar(out=out, in0=in0, scalar1=s1, scalar2=s2, op0=op0, **kw),
                 [out, accum_out], [in0, s1, s2])

    def stt(self, out, in0, scalar, in1, op0, op1, eng="dve"):
        self.add(eng, lambda e: e.scalar_tensor_tensor(out=out, in0=in0, scalar=scalar, in1=in1, op0=op0, op1=op1),
                 [out], [in0, scalar, in1])

    def copy(self, out, in_, eng="dve"):
        if eng == "act":
            self.add("act", lambda e: e.copy(out=out, in_=in_), [out], [in_])
        else:
            self.add(eng, lambda e: e.tensor_copy(out=out, in_=in_), [out], [in_])

    def memset(self, ap, val, eng="dve"):
        self.add(eng, lambda e: e.memset(ap, val), [ap], [])

    def recip(self, out, in_):
        self.add("dve", lambda e: e.reciprocal(out=out, in_=in_), [out], [in_])

    def scan(self, out, d0, d1, initial, op0=ALU.mult, op1=ALU.add):
        self.add("dve", lambda e: e.tensor_tensor_scan(out=out, data0=d0, data1=d1, initial=initial, op0=op0, op1=op1),
                 [out], [d0, d1, initial])

    def mm(self, out, pairs, start=True, stop=True):
        pairs = list(pairs)
        n = len(pairs)

        def fn(e):
            ins = None
            for i, (l, r) in enumerate(pairs):
                ins = e.matmul(out, l, r, start=(start and i == 0), stop=(stop and i == n - 1))
            return ins
        self.add("pe", fn, [out], [a for p in pairs for a in p])

    def transpose(self, out, in_, ident):
        self.add("pe", lambda e: e.transpose(out, in_, ident), [out], [in_, ident])


from contextlib import ExitStack
from concourse.bass_utils import run_bass_kernel_spmd

U8 = mybir.dt.uint8
T = 1024
KC = 16
G = 256
NG_MAIN = 34
NG_GATE = 24
COL_AQ, COL_AF, COL_AI, COL_AZ, COL_BX, COL_BZ, COL_CZ, COL_CX, COL_DT, COL_G = (
    0, 1024, 2048, 3072, 4096, 5120, 6144, 7168, 8704, 8720)
EPS = 1e-6

PP_LBL, PP_FNW, PP_L0 = 0, 32, 48
PL = 156
O_NW, O_HNW, O_RCW, O_RCB, O_BA, O_BX, O_LAM, O_SCW, O_SCB, O_SNW = 0, 16, 24, 56, 64, 72, 80, 88, 136, 148
NP = PP_L0 + 4 * PL
RB_L = 48
NR = 4 * RB_L
C_ID, C_TRI, C_MA, C_NEG, NCST = 0, 128, 256, 384, 512


def fm(v, n):
    return np.ascontiguousarray(v.reshape(n, 128).T)


def host_consts():
    c = np.zeros((128, NCST), np.float32)
    i = np.arange(128)
    c[:, C_ID:C_ID + 128] = np.eye(128, dtype=np.float32)
    c[:, C_TRI:C_TRI + 128] = (i[:, None] <= i[None, :]).astype(np.float32)
    c[:, C_MA:C_MA + 128] = ((i[:, None] <= i[None, :]) & (i[:, None] // 64 == i[None, :] // 64)).astype(np.float32)
    c[:, C_NEG:C_NEG + 128] = np.where(i[None, :] < i[:, None], -30000.0, 0.0).astype(np.float32)
    return c


def host_prep(inp, L=4):
    f32 = np.float32
    pp = np.zeros((128, NP), f32)
    lbl = inp["hgrn_lb_logits"].astype(f32)
    pp[:, PP_LBL:PP_LBL + 32] = lbl.reshape(4, 8, 128).transpose(2, 0, 1).reshape(128, 32)
    pp[:, PP_FNW:PP_FNW + 16] = fm(inp["final_norm_w"].astype(f32), 16)
    rb = np.zeros((128, NR), f32)
    for l in range(4):
        b = PP_L0 + l * PL
        pp[:, b + O_NW:b + O_NW + 16] = fm(inp["norm_w"][l], 16)
        pp[:, b + O_HNW:b + O_HNW + 8] = fm(inp["hgrn_norm_w"][l], 8)
        for k in range(4):
            pp[:, b + O_RCW + k * 8:b + O_RCW + k * 8 + 8] = fm(inp["rglru_conv_w"][l, k], 8)
            pp[:, b + O_SCW + k * 12:b + O_SCW + k * 12 + 12] = fm(inp["ssd_conv_w"][l, k], 12)
        pp[:, b + O_RCB:b + O_RCB + 8] = fm(inp["rglru_conv_b"][l], 8)
        pp[:, b + O_BA:b + O_BA + 8] = fm(inp["rglru_ba"][l].reshape(-1), 8)
        pp[:, b + O_BX:b + O_BX + 8] = fm(inp["rglru_bx"][l].reshape(-1), 8)
        pp[:, b + O_LAM:b + O_LAM + 8] = fm(inp["rglru_lambda"][l], 8)
        pp[:, b + O_SCB:b + O_SCB + 12] = fm(inp["ssd_conv_b"][l], 12)
        pp[:, b + O_SNW:b + O_SNW + 8] = fm(inp["ssd_norm_w"][l], 8)
        rb[:, l * RB_L + 0:l * RB_L + 16] = inp["ssd_dt_bias"][l][None, :]
        rb[:, l * RB_L + 16:l * RB_L + 32] = inp["ssd_a_log"][l][None, :]
        rb[:, l * RB_L + 32:l * RB_L + 48] = inp["ssd_d"][l][None, :]
    w_in = inp["w_in"]
    wmain = np.ascontiguousarray(
        w_in[:L, :, :COL_DT].reshape(L, KC, 128, NG_MAIN, G).transpose(0, 3, 2, 1, 4))
    wdt = np.ascontiguousarray(w_in[:L, :, COL_DT:COL_G].reshape(L, KC, 128, 16).transpose(0, 2, 1, 3))
    wgate = np.ascontiguousarray(
        w_in[:L, :, COL_G:].reshape(L, KC, 128, NG_GATE, G).transpose(0, 3, 2, 1, 4))
    wout = np.ascontiguousarray(
        inp["w_out"][:L].reshape(L, 3, 8, 128, 8, G).transpose(0, 1, 4, 3, 2, 5))
    wab = np.ascontiguousarray(np.stack([inp["rglru_wa"][:L], inp["rglru_wx"][:L]], 1).transpose(0, 3, 1, 2, 4))
    shared = dict(pp=pp, rb=rb, cst=host_consts(), wmain=wmain, wdt=wdt, wgate=wgate, wout=wout, wab=wab)
    x = inp["x"]
    percore = []
    for r in range(8):
        b, s = r // 4, r % 4
        xs = x[b, s * T:(s + 1) * T, :]
        xT = np.ascontiguousarray(xs.reshape(T, KC, 128).transpose(2, 1, 0))
        xh = np.zeros((128, KC, 4), f32)
        if s > 0:
            xh[:, :, 0:3] = x[b, s * T - 3:s * T, :].reshape(3, KC, 128).transpose(2, 1, 0)
        sel = np.zeros((128, 8), f32)
        for m in range(4):
            sel[:, m] = 1.0 if m < s else 0.0
            sel[:, 4 + m] = 1.0 if m == s - 1 else 0.0
        percore.append(dict(xT=xT, xh=xh, sel=sel))
    return shared, percore


class Prog:
    def __init__(self, L=4, branches="BAC", dbg=False, generic=False):
        self.generic = generic
        self.L = L
        self.branches = branches
        self.dbg = dbg
        self.stack = ExitStack()
        self.nc = bass.Bass("TRN2", target_bir_lowering=False)
        self.S = Sched(self.nc, self.stack)
        self.build()

    def build(self):
        S, nc, L = self.S, self.nc, self.L
        D = S.dram
        self.d_xT = D("xT", [128, KC, T], F32, kind="ExternalInput", track=False)
        self.d_xh = D("xh", [128, KC, 4], F32, kind="ExternalInput", track=False)
        self.d_sel = D("sel", [128, 8], F32, kind="ExternalInput", track=False)
        self.d_pp = D("pp", [128, NP], F32, kind="ExternalInput", track=False)
        self.d_rb = D("rb", [128, NR], F32, kind="ExternalInput", track=False)
        self.d_cst = D("cst", [128, NCST], F32, kind="ExternalInput", track=False)
        self.d_wmain = D("wmain", [L, NG_MAIN, 128, KC * G], F32, kind="ExternalInput", track=False)
        self.d_wdt = D("wdt", [L, 128, KC * 16], F32, kind="ExternalInput", track=False)
        self.d_wgate = D("wgate", [L, NG_GATE, 128, KC * G], F32, kind="ExternalInput", track=False)
        self.d_wout = D("wout", [L, 3, 8, 128, 8 * G], F32, kind="ExternalInput", track=False)
        self.d_wab = D("wab", [L, 128, 2 * 8 * 128], F32, kind="ExternalInput", track=False)
        self.d_out = D("out", [128, KC, T], F32, kind="ExternalOutput")
        self.d_xout = D("xout", [128, KC, T], F32, kind="ExternalOutput")
        self.d_mlb = D("mlb", [128, 4], F32, kind="ExternalInput", track=False)
        if self.dbg:
            self.d_dbg = D("dbg", [3, 128, 8, T], BF16, kind="ExternalOutput", blk=128 * 8 * T)

        self.xT = S.sb("xT_s", [128, KC, T], F32, blk=512)
        self.hT = S.sb("hT_s", [128, KC, T], BF16, blk=512)
        self.hh = S.sb("hh_s", [128, KC, 4], BF16)
        self.xh = S.sb("xh_s", [128, KC, 4], F32)
        self.oT = S.sb("oT_s", [128, 8, T], BF16, blk=512)
        self.NBUF = 2
        self.wbuf = [S.sb("wbuf%d" % i, [128, KC * G], BF16) for i in range(self.NBUF)]
        self.wslot = 0
        self.pp = S.sb("pp_s", [128, NP], F32)
        self.rb = S.sb("rb_s", [128, NR], F32)
        self.sel = S.sb("sel_s", [128, 8], F32)
        self.mlb = S.sb("mlb_s", [128, 4], F32)
        self.ident_bf = S.sb("ident_bf", [128, 128], BF16)
        self.ones_bf = S.sb("ones_bf", [128, 128], BF16)
        self.lb = S.sb("lb_s", [128, 4, 8], F32)
        self.omlb = S.sb("omlb_s", [128, 4, 8], F32)
        self.c8 = S.sb("c8_s", [128, 4, 8], F32)
        self.c16 = S.sb("c16_s", [128, 4, 8], F32)
        self.tiny = S.sb("tiny_s", [128, 256], F32, blk=32)
        self.NBLK = 62
        self.arena = S.sb("arena", [128, self.NBLK * 256], F32, blk=256)
        self.scr = [self.af(0, 5)[:, 0:1028]] + [self.af(5 + 4 * i, 4) for i in range(8)]
        self.bscr = [self.ab(37 + 2 * i, 2) for i in range(2)]
        self.oca = self.ab(41, 16).rearrange("p (h t) -> p h t", h=8)
        self.wab = self.ab(57, 4).rearrange("p (a h j) -> p a h j", a=2, h=8)
        self.cst = self.af(0, 2)
        self.cmask = S.sb("cmask_s", [128, T], U8)
        self.maskA = S.sb("maskA_s", [128, 128], U8)
        self.tri_bf = S.sb("tri_bf", [128, 128], BF16)
        self.ntri_bf = S.sb("ntri_bf", [128, 128], BF16)
        self.negrep = S.sb("negrep_s", [128, 4, 128], BF16)
        self.scTm = [S.sb("scTm%d" % i, [128, 128], BF16) for i in range(2)]
        self.Sst = S.sb("Sst_s", [128, 128], F32)
        self.Sbf = [S.sb("Sbf%d" % i, [128, 128], BF16) for i in range(2)]
        self.small = S.sb("small_s", [128, 1792], F32, blk=128)
        self.wdt_s = self.ab(61, 1)[:, 0:KC * 16]
        self.psum = S.ps("psum", [128, 4096], F32, blk=512)
        self.pbank = 0
        self.cc_id = 0

        ident_f = self.cst[:, C_ID:C_ID + 128]
        S.dma("sp", "ld_x", self.xT[:], self.d_xT)
        S.dma("sp", "ld_xh", self.xh[:], self.d_xh)
        S.dma("sp", "ld_pp", self.pp[:], self.d_pp)
        S.dma("sp", "ld_rb", self.rb[:], self.d_rb)
        S.dma("sp", "ld_sel", self.sel[:], self.d_sel)
        S.dma("sp", "ld_mlb", self.mlb[:], self.d_mlb)
        S.dma("sp", "ld_cst", self.cst[:], self.d_cst)
        S.copy(self.ident_bf[:], ident_f)
        S.memset(self.ones_bf[:], 1.0)
        S.memset(self.cmask[:], 1.0)
        S.memset(self.cmask[:].rearrange("p (c k) -> p c k", k=64)[:, :, 0:1], 0.0)
        S.copy(self.maskA[:], self.cst[:, C_MA:C_MA + 128])
        S.copy(self.tri_bf[:], self.cst[:, C_TRI:C_TRI + 128])
        S.ts(self.ntri_bf[:], self.cst[:, C_TRI:C_TRI + 128], -1.0, None, ALU.mult)
        S.copy(self.negrep[:], self.cst[:, C_NEG:C_NEG + 128].unsqueeze(1).to_broadcast([128, 4, 128]))
        for i in range(2):
            S.memset(self.scTm[i][:], 0.0)
        self.setup_params()
        for l in range(L):
            self.layer(l)
        self.final()
        S.wait_all("sp")
        S.emit()

    def setup_params(self):
        S = self.S
        tv = self.tiny
        lg = self.pp[:, PP_LBL:PP_LBL + 32].rearrange("p (l h) -> p l h", l=4)
        m = tv[:, 0:8]
        S.tt(m, lg[:, 0, :], lg[:, 1, :], ALU.max)
        S.tt(m, m, lg[:, 2, :], ALU.max)
        S.tt(m, m, lg[:, 3, :], ALU.max)
        e = tv[:, 32:64].rearrange("p (l h) -> p l h", l=4)
        S.tt(e, lg, m.unsqueeze(1).to_broadcast([128, 4, 8]), ALU.subtract)
        S.act(e, e, AF.Exp)
        ss = tv[:, 8:16]
        S.tt(ss, e[:, 0, :], e[:, 1, :], ALU.add)
        S.tt(ss, ss, e[:, 2, :], ALU.add)
        S.tt(ss, ss, e[:, 3, :], ALU.add)
        S.recip(ss, ss)
        S.tt(e, e, ss.unsqueeze(1).to_broadcast([128, 4, 8]), ALU.mult)
        S.memset(self.lb[:, 0, :], 0.0)
        S.copy(self.lb[:, 1, :], e[:, 1, :])
        S.tt(self.lb[:, 2, :], e[:, 1, :], e[:, 2, :], ALU.add)
        S.tt(self.lb[:, 3, :], self.lb[:, 2, :], e[:, 3, :], ALU.add)
        if self.generic:
            S.ts(self.lb[:, 0, :], e[:, 0, :], self.mlb[:, 0:1], None, ALU.mult)
            for i in range(1, 4):
                S.stt(self.lb[:, 0, :], e[:, i, :], self.mlb[:, i:i + 1], self.lb[:, 0, :], ALU.mult, ALU.add)
        S.ts(self.omlb[:], self.lb[:], -1.0, 1.0, ALU.mult, ALU.add)
        for l in range(self.L):
            lam = self.pp[:, PP_L0 + l * PL + O_LAM:PP_L0 + l * PL + O_LAM + 8]
            t = tv[:, 64:72]
            S.act(t, lam, AF.Exp, scale=-1.0)
            S.act(t, t, AF.Ln, bias=1.0)
            S.ts(self.c8[:, l, :], t, -8.0, None, ALU.mult)
            S.ts(self.c16[:, l, :], t, -16.0, None, ALU.mult)

    def dump(self, name, ap):
        if not self.dbg:
            return
        if not hasattr(self, "dumps"):
            self.dumps = {}
        if name in self.dumps:
            return
        d = self.S.dram("dump_" + name, list(ap.shape), ap.dtype, kind="ExternalOutput")
        self.dumps[name] = d
        self.S.dma("sp", "dump", d, ap)

    def af(self, b0, nb):
        return self.arena[:, b0 * 256:(b0 + nb) * 256]

    def ab(self, b0, nb):
        return self.arena[:, b0 * 256:(b0 + nb) * 256].bitcast(BF16)

    def P(self, l, off, n=1):
        b = PP_L0 + l * PL + off
        return self.pp[:, b:b + n]

    def bank(self, n=1):
        b = self.pbank
        if b + n > 6:
            b = 0
        self.pbank = (b + n) % 6
        return self.psum[:, b * 512:(b + n) * 512]

    def load_w(self, src, ncols_total):
        slot = self.wslot
        self.wslot = (self.wslot + 1) % self.NBUF
        dst = self.wbuf[slot][:, 0:ncols_total]
        self.S.dma("pool", "w%d" % slot, dst, src, max_dma_last_dim=8192)
        return self.wbuf[slot]

    def wmain(self, l, gi):
        w = self.load_w(self.d_wmain[l, gi], KC * G)
        return w[:, :].rearrange("p (k c) -> p k c", k=KC)

    def wgate(self, l, gi):
        w = self.load_w(self.d_wgate[l, gi], KC * G)
        return w[:, :].rearrange("p (k c) -> p k c", k=KC)

    def wout(self, l, n, g):
        w = self.load_w(self.d_wout[l, n, g], 8 * G)
        return w[:, 0:8 * G].rearrange("p (k c) -> p k c", k=8)

    def proj_fm(self, w, j, halo_ps=None):
        S = self.S
        ps = self.bank(2)
        hT, hh = self.hT, self.hh

        def fn(e):
            ins = None
            for half in range(2):
                for kc in range(KC):
                    ins = e.matmul(ps[:, half * 512:(half + 1) * 512], w[:, kc, j * 128:(j + 1) * 128],
                                   hT[:, kc, half * 512:(half + 1) * 512], start=(kc == 0), stop=(kc == KC - 1))
            if halo_ps is not None:
                for kc in range(KC):
                    ins = e.matmul(halo_ps, w[:, kc, j * 128:(j + 1) * 128], hh[:, kc, 0:4],
                                   start=(kc == 0), stop=(kc == KC - 1))
            return ins
        outs = [ps] + ([halo_ps] if halo_ps is not None else [])
        S.add("pe", fn, outs, [w[:, :, j * 128:(j + 1) * 128], hT[:], hh[:]])
        return ps

    def layer(self, l):
        S = self.S
        self.rmsnorm_in(l)
        if l > 0 or True:
            pass
        S.dma("pool", "ld_wab", self.ab(57, 4), self.d_wab[l], max_dma_last_dim=8192)
        for br in self.branches:
            if br == "B":
                self.branch_B(l)
                self.out_proj(l, 1)
            elif br == "A":
                self.branch_A(l)
                self.out_proj(l, 0)
            elif br == "C":
                self.branch_C(l)
                self.out_proj(l, 2)
        if l + 1 < self.L:
            self.halo_exchange(l)

    def rmsnorm_in(self, l):
        S = self.S
        xT, hT = self.xT, self.hT
        for half in range(2):
            sl = slice(half * 512, (half + 1) * 512)
            ps = self.bank(1)
            sqs = []
            for kc in range(KC):
                sq = self.bscr[kc % 2][:, 0:512]
                S.act(sq, xT[:, kc, sl], AF.Square)
                S.mm(ps, [(self.ones_bf[:], sq)], start=(kc == 0), stop=(kc == KC - 1))
            rstd = self.scr[0][:, 0:512]
            S.act(rstd, ps, AF.Sqrt, bias=EPS, scale=1.0 / 2048.0)
            S.recip(rstd, rstd)
            for kc in range(KC):
                S.stt(hT[:, kc, sl], xT[:, kc, sl], self.P(l, O_NW + kc), rstd, ALU.mult, ALU.mult)
        xh = self.xh
        sqh = self.bscr[0][:, 0:64].rearrange("p (k c) -> p k c", k=KC)
        S.act(sqh, xh[:], AF.Square)
        psh = self.psum[:, 7 * 512:7 * 512 + 4]
        S.mm(psh, [(self.ones_bf[:], sqh[:, kc, :]) for kc in range(KC)])
        rh = self.tiny[:, 96:100]
        S.act(rh, psh, AF.Sqrt, bias=EPS, scale=1.0 / 2048.0)
        S.recip(rh, rh)
        th = self.scr[0][:, 512:576].rearrange("p (k c) -> p k c", k=KC)
        S.tt(th, xh[:], self.P(l, O_NW, 16).unsqueeze(2).to_broadcast([128, KC, 4]), ALU.mult)
        S.tt(self.hh[:], th, rh.unsqueeze(1).to_broadcast([128, KC, 4]), ALU.mult)

    def exchange(self, src_sb, width, name):
        S = self.S
        i = self.cc_id
        self.cc_id += 1
        if not hasattr(self, "ccbufs"):
            self.ccbufs = {}
        if name not in self.ccbufs:
            self.ccbufs[name] = (S.dram("ccin_" + name, [128, width], F32),
                                 S.dram("ccout_" + name, [4 * 128, width], F32))
        bin_, bout = self.ccbufs[name]
        if isinstance(src_sb, list):
            for (src, off, w) in src_sb:
                S.dma("sp", "cc_st_" + name, bin_[:, off:off + w], src)
        else:
            S.dma("sp", "cc_st_" + name, bin_, src_sb)
        S.coll("cc%d" % i, "AllGather", [[0, 1, 2, 3], [4, 5, 6, 7]], bin_, bout)
        return bout

    def halo_exchange(self, l):
        S = self.S
        src = self.scr[1][:, 0:64]
        S.copy(src.rearrange("p (k c) -> p k c", k=KC), self.xT[:, :, T - 4:T])
        bout = self.exchange(src, 64, "halo")
        g = self.scr[2][:, 0:256]
        S.dma("sp", "cc_ld_halo", g.rearrange("p (r c) -> p r c", r=4), bout.rearrange("(r p) c -> p r c", p=128))
        gv = g.rearrange("p (r k c) -> p r k c", r=4, k=KC)
        xh = self.xh
        S.ts(xh[:, :, 0:3], gv[:, 0, :, 1:4], self.sel[:, 4:5], None, ALU.mult)
        for m in range(1, 4):
            S.stt(xh[:, :, 0:3], gv[:, m, :, 1:4], self.sel[:, 4 + m:5 + m], xh[:, :, 0:3], ALU.mult, ALU.add)

    def branch_B(self, l):
        S = self.S
        xb_, xc_, r_, ig_, a2_, hl_, ca_, sz_ = [self.scr[i] for i in range(8)]
        zero = self.scr[8][:, 0:T]
        S.memset(zero, 0.0)
        hlast = self.tiny[:, 128:136]
        atot = self.tiny[:, 136:144]
        wx_list = {}
        for h in range(8):
            gi, j = (COL_BX // G) + h // 2, h % 2
            if j == 0:
                wcur = self.wmain(l, gi)
            halo_ps = self.psum[:, 6 * 512:6 * 512 + 4]
            ps = self.proj_fm(wcur, j, halo_ps)
            xb = xb_[:, 0:T + 3]
            S.copy(xb[:, 3:T + 3], ps, eng="act")
            S.copy(xb[:, 0:3], halo_ps[:, 0:3], eng="act")
            xc = xc_[:, 0:T]
            S.ts(xc, xb[:, 0:T], self.P(l, O_RCW + 0 * 8 + h), self.P(l, O_RCB + h), ALU.mult, ALU.add)
            for k in range(1, 4):
                S.stt(xc, xb[:, k:k + T], self.P(l, O_RCW + k * 8 + h), xc, ALU.mult, ALU.add)
            xcb = self.bscr[0][:, 0:T]
            S.copy(xcb, xc, eng="act")
            if h == 0 and l == 0:
                self.dump("hT0", self.hT[:, 0, :]); self.dump("xb", xb); self.dump("xc", xc); self.dump("hh", self.hh[:])
            r = r_[:, 0:T]
            ig = ig_[:, 0:T]
            for (dst, which, boff) in ((r, 0, O_BA), (ig, 1, O_BX)):
                psg = self.bank(2)
                for half in range(2):
                    S.mm(psg[:, half * 512:(half + 1) * 512], [(self.wab[:, which, h, :], xcb[:, half * 512:(half + 1) * 512])])
                S.act(dst, psg, AF.Sigmoid, bias=self.P(l, boff + h))
            a2 = a2_[:, 0:T]
            S.act(a2, r, AF.Exp, scale=self.c16[:, l, h:h + 1])
            S.act(r, r, AF.Exp, scale=self.c8[:, l, h:h + 1])
            S.ts(a2, a2, -1.0, 1.0, ALU.mult, ALU.add)
            S.act(a2, a2, AF.Sqrt)
            S.tt(ig, ig, xc, ALU.mult)
            S.tt(a2, a2, ig, ALU.mult)
            hl = hl_[:, 0:T]
            ca = ca_[:, 0:T]
            if h == 0 and l == 0:
                self.dump("a", r); self.dump("u", a2); self.dump("igx", ig)
            S.scan(hl, r, a2, 0.0)
            S.scan(ca, r, zero, 1.0)
            S.copy(hlast[:, h:h + 1], hl[:, T - 1:T])
            S.copy(atot[:, h:h + 1], ca[:, T - 1:T])
            gi, j = (COL_BZ // G) + h // 2, h % 2
            if j == 0:
                wz = self.wmain(l, gi)
            psz = self.proj_fm(wz, j)
            sz = sz_[:, 0:T]
            S.act(sz, psz, AF.Silu)
            if h == 0 and l == 0:
                self.dump("hl", hl); self.dump("ca", ca); self.dump("sz", sz)
            S.tt(self.oT[:, h, :], hl, sz, ALU.mult)
            S.tt(self.oca[:, h, :], ca, sz, ALU.mult)
        src = self.tiny[:, 128:144]
        bout = self.exchange(src, 16, "B")
        g = self.tiny[:, 160:224]
        S.dma("sp", "cc_ld_B", g.rearrange("p (r c) -> p r c", r=4), bout.rearrange("(r p) c -> p r c", p=128))
        gv = g.rearrange("p (r c) -> p r c", r=4)
        hin = self.tiny[:, 144:152]
        self.horner(hin, lambda m: gv[:, m, 8:16], lambda m: gv[:, m, 0:8], 8)
        for h in range(8):
            S.stt(self.oT[:, h, :], self.oca[:, h, :], hin[:, h:h + 1], self.oT[:, h, :], ALU.mult, ALU.add)

    def horner(self, tacc, dec, val, width):
        S = self.S
        tmp = self.tiny[:, 224:224 + width]
        tmp2 = self.tiny[:, 240:240 + width]
        S.memset(tacc, 0.0)
        for m in range(3):
            pm = self.sel[:, m:m + 1]
            S.ts(tmp, dec(m), -1.0, pm, ALU.add, ALU.mult)
            S.ts(tmp, tmp, 1.0, None, ALU.add)
            S.tt(tacc, tacc, tmp, ALU.mult)
            S.ts(tmp2, val(m), pm, None, ALU.mult)
            S.tt(tacc, tacc, tmp2, ALU.add)

    def out_proj(self, l, n):
        S = self.S
        if self.dbg and l == 0:
            S.dma("sp", "dbg", self.d_dbg[n], self.oT[:])
        for g in range(8):
            wg = self.wgate(l, n * 8 + g)
            wo = self.wout(l, n, g)
            for j in range(2):
                dc = g * 2 + j
                for half in range(2):
                    sl = slice(half * 512, (half + 1) * 512)
                    psg = self.bank(1)
                    S.mm(psg, [(wg[:, kc, j * 128:(j + 1) * 128], self.hT[:, kc, sl]) for kc in range(KC)])
                    gate = self.scr[half][:, 0:512]
                    S.act(gate, psg, AF.Sigmoid)
                    pso = self.bank(1)
                    S.mm(pso, [(wo[:, cc, j * 128:(j + 1) * 128], self.oT[:, cc, sl]) for cc in range(8)])
                    S.tt(gate, pso, gate, ALU.mult)
                    S.tt(self.xT[:, dc, sl], self.xT[:, dc, sl], gate, ALU.add)

    def final(self):
        S = self.S
        xT = self.xT
        for kc in range(KC):
            S.dma("sp", "st_xout", self.d_xout[:, kc, :], xT[:, kc, :])
        for half in range(2):
            sl = slice(half * 512, (half + 1) * 512)
            ps = self.bank(1)
            for kc in range(KC):
                sq = self.bscr[kc % 2][:, 0:512]
                S.act(sq, xT[:, kc, sl], AF.Square)
                S.mm(ps, [(self.ones_bf[:], sq)], start=(kc == 0), stop=(kc == KC - 1))
            rstd = self.scr[0][:, 0:512]
            S.act(rstd, ps, AF.Sqrt, bias=EPS, scale=1.0 / 2048.0)
            S.recip(rstd, rstd)
            for kc in range(KC):
                S.stt(xT[:, kc, sl], xT[:, kc, sl], self.pp[:, PP_FNW + kc:PP_FNW + kc + 1], rstd, ALU.mult, ALU.mult)
        for kc in range(KC):
            S.dma("sp", "st_out", self.d_out[:, kc, :], xT[:, kc, :])

    def proj_tm(self, w, ncols, tc0, ps):
        hT = self.hT

        def fn(e):
            ins = None
            for i in range(2):
                tc = tc0 + i
                for kc in range(KC):
                    ins = e.matmul(ps[:, i * ncols:(i + 1) * ncols], hT[:, kc, tc * 128:(tc + 1) * 128],
                                   w[:, kc, 0:ncols], start=(kc == 0), stop=(kc == KC - 1))
            return ins
        self.S.add("pe", fn, [ps[:, 0:2 * ncols]], [w[:, :, 0:ncols], hT[:]])

    def branch_A(self, l):
        S = self.S
        vtok = self.ab(0, 16).rearrange("p (c v) -> p c v", c=8)
        qc = self.ab(16, 16).rearrange("p (h t) -> p h t", h=8)
        Lt, KK, Bc, BM, E2 = [self.af(32 + 4 * i, 4) for i in range(5)]
        qg, kg, kgT = [self.ab(52 + 2 * i, 2) for i in range(3)]
        stA = self.af(58, 4)
        dtotA = self.small[:, 416:424]
        sm = self.small
        for g in range(4):
            w = self.wmain(l, COL_AI // G + g)
            for tc0 in range(0, 8, 2):
                ps = self.bank(1)
                self.proj_tm(w, G, tc0, ps)
                S.copy(vtok[:, tc0:tc0 + 2, g * G:(g + 1) * G], ps.rearrange("p (i c) -> p i c", i=2), eng="act")
        c3 = lambda ap: ap.rearrange("p (c k) -> p c k", k=64)
        for h in range(8):
            j = h % 2
            if j == 0:
                wq = self.wmain(l, COL_AQ // G + h // 2)
                wf = self.wmain(l, COL_AF // G + h // 2)
            psf = self.proj_fm(wf, j)
            S.act(Lt, psf, AF.Sigmoid)
            S.act(KK, psf, AF.Sigmoid, scale=-1.0)
            psq = self.proj_fm(wq, j)
            S.ts(Lt, Lt, self.omlb[:, l, h:h + 1], self.lb[:, l, h:h + 1], ALU.mult, ALU.add)
            S.act(Lt, Lt, AF.Ln)
            S.scan(Bc, self.cmask[:], Lt, 0.0)
            S.tt(c3(BM), c3(Bc), c3(Bc)[:, :, 31:32].to_broadcast([128, 16, 64]), ALU.subtract)
            eR, eL, eT, eP, blast, incl, eRP = [sm[:, 16 * i:16 * i + 16] for i in range(7)]
            S.act(eR, c3(Bc)[:, :, 31], AF.Exp)
            S.act(eL, c3(BM)[:, :, 63], AF.Exp)
            S.act(eT, c3(Bc)[:, :, 63], AF.Exp)
            S.copy(blast, c3(Bc)[:, :, 63])
            S.scan(incl, sm[:, 128:144], blast, 0.0) if False else None
            S.memset(sm[:, 128:144], 1.0)
            S.scan(incl, sm[:, 128:144], blast, 0.0)
            S.act(dtotA[:, h:h + 1], incl[:, 15:16], AF.Exp)
            S.tt(eP, incl, blast, ALU.subtract)
            S.act(eP, eP, AF.Exp)
            S.tt(eRP, eR, eP, ALU.mult)
            S.act(E2, BM, AF.Exp, scale=-1.0)
            S.act(BM, BM, AF.Exp)
            S.tt(qg, psq, BM, ALU.mult)
            S.stt(kg, KK, self.omlb[:, l, h:h + 1], E2, ALU.mult, ALU.mult)
            S.tt(c3(qc[:, h, :]), c3(qg), eRP.unsqueeze(2).to_broadcast([128, 16, 64]), ALU.mult)
            pst = self.bank(1).bitcast(BF16)
            for tc in range(8):
                S.transpose(pst[:, tc * 128:(tc + 1) * 128], kg[:, tc * 128:(tc + 1) * 128], self.ident_bf[:])
            S.copy(kgT, pst, eng="act")
            Sst = self.Sst
            for tc in range(8):
                tsl = slice(tc * 128, (tc + 1) * 128)
                psS = self.bank(1)[:, 0:128]
                S.mm(psS, [(kg[:, tsl], qg[:, tsl])])
                sc = self.scTm[tc % 2]
                S.add("dve", lambda e, sc=sc, psS=psS: e.copy_predicated(out=sc[:], mask=self.maskA[:], data=psS),
                      [sc[:]], [self.maskA[:], psS])
                pso = self.bank(1)[:, 0:128]
                S.mm(pso, [(vtok[:, tc, h * 128:(h + 1) * 128], sc[:])], start=True, stop=False)
                for i in range(2):
                    c = 2 * tc + i
                    rows = slice(i * 64, (i + 1) * 64)
                    if c > 0:
                        sb_ = self.Sbf[c % 2]
                        S.ts(sb_[:], Sst[:], eR[:, c:c + 1], None, ALU.mult)
                        S.mm(pso[:, i * 64:(i + 1) * 64], [(sb_[:], qg[:, c * 64:(c + 1) * 64])], start=False, stop=(i == 1))
                    elif True:
                        pass
                    psU = self.bank(1)[:, 0:128]
                    S.mm(psU, [(kgT[rows, tc * 128:(tc + 1) * 128], vtok[rows, tc, h * 128:(h + 1) * 128])])
                    if c == 0:
                        S.ts(Sst[:], psU, eL[:, c:c + 1], None, ALU.mult)
                    else:
                        tmpU = sm[:, 256:384]
                        S.ts(tmpU, psU, eL[:, c:c + 1], None, ALU.mult)
                        S.stt(Sst[:], Sst[:], eT[:, c:c + 1], tmpU, ALU.mult, ALU.add)
                S.copy(self.oT[:, h, tsl], pso, eng="act")
            S.copy(stA[:, h * 128:(h + 1) * 128], Sst[:])
        bout = self.exchange([(stA, 0, 1024), (dtotA, 1024, 8)], 1032, "A")
        gth = self.af(32, 13)[:, 0:3 * 1032].rearrange("p (r c) -> p r c", r=3)
        S.dma("sp", "cc_ld_A", gth, bout.rearrange("(r p) c -> p r c", p=128)[:, 0:3, :])
        Tin = self.af(45, 4)
        S.memset(Tin, 0.0)
        dpr = sm[:, 400:408]
        for m in range(3):
            pm = self.sel[:, m:m + 1]
            S.ts(dpr, gth[:, m, 1024:1032], -1.0, pm, ALU.add, ALU.mult)
            S.ts(dpr, dpr, 1.0, None, ALU.add)
            S.tt(Tin.rearrange("p (h v) -> p h v", h=8), Tin.rearrange("p (h v) -> p h v", h=8),
                 dpr.unsqueeze(2).to_broadcast([128, 8, 128]), ALU.mult)
            S.stt(Tin, gth[:, m, 0:1024], pm, Tin, ALU.mult, ALU.add)
        Sin = self.ab(49, 2)
        S.copy(Sin, Tin, eng="act")
        O_, sz_, rs_ = self.af(53, 4), self.af(57, 4), self.af(32, 4)
        sqb = self.ab(51, 2)
        for h in range(8):
            j = h % 2
            if j == 0:
                wz = self.wmain(l, COL_AZ // G + h // 2)
            psz = self.proj_fm(wz, j)
            S.act(sz_, psz, AF.Silu)
            psc = self.bank(2)
            for half in range(2):
                sl = slice(half * 512, (half + 1) * 512)
                S.mm(psc[:, sl], [(Sin[:, h * 128:(h + 1) * 128], qc[:, h, sl])])
            S.tt(O_, psc, self.oT[:, h, :], ALU.add)
            S.act(sqb, O_, AF.Square)
            psn = self.bank(2)
            for half in range(2):
                sl = slice(half * 512, (half + 1) * 512)
                S.mm(psn[:, sl], [(self.ones_bf[:], sqb[:, sl])])
            S.act(rs_, psn, AF.Sqrt, bias=EPS, scale=1.0 / 128.0)
            S.recip(rs_, rs_)
            S.stt(O_, O_, self.P(l, O_HNW + h), rs_, ALU.mult, ALU.mult)
            S.tt(self.oT[:, h, :], O_, sz_, ALU.mult)

    def branch_C(self, l):
        S = self.S
        sm = self.small
        xs_tok = self.ab(0, 16).rearrange("p (c v) -> p c v", c=8)
        szt = self.ab(16, 16).rearrange("p (c v) -> p c v", c=8)
        BT = self.ab(32, 4).rearrange("p (g t) -> p g t", g=2)
        CT = self.ab(36, 4).rearrange("p (g t) -> p g t", g=2)
        Btok = self.ab(40, 4).rearrange("p (c g n) -> p c g n", c=8, g=2)
        xb = self.af(50, 5)[:, 0:T + 3]
        xc = self.af(55, 4)
        xsb = self.ab(59, 2)
        S.dma("pool", "ld_wdt", self.wdt_s, self.d_wdt[l], max_dma_last_dim=8192)
        wdt = self.wdt_s.rearrange("p (k c) -> p k c", k=KC)
        psdt = self.bank(1)[:, 0:128]
        hT = self.hT

        def fn(e):
            ins = None
            for tc in range(8):
                for kc in range(KC):
                    ins = e.matmul(psdt[:, tc * 16:(tc + 1) * 16], hT[:, kc, tc * 128:(tc + 1) * 128], wdt[:, kc, :],
                                   start=(kc == 0), stop=(kc == KC - 1))
            return ins
        S.add("pe", fn, [psdt], [wdt, hT[:]])
        rbl = self.rb[:, l * RB_L:(l + 1) * RB_L]
        v3 = lambda ap: ap.rearrange("p (c h) -> p c h", c=8)
        dt, da, dahi, cum, tot, ecum, decs, edec = [sm[:, 512 + 128 * i:640 + 128 * i] for i in range(8)]
        arep = sm[:, 448:464]
        S.tt(v3(dt), v3(psdt), rbl[:, 0:16].unsqueeze(1).to_broadcast([128, 8, 16]), ALU.add)
        S.act(dt, dt, AF.Exp)
        S.act(dt, dt, AF.Ln, bias=1.0)
        S.act(arep, rbl[:, 16:32], AF.Exp)
        S.ts(arep, arep, -1.0, None, ALU.mult)
        S.tt(v3(da), v3(dt), arep.unsqueeze(1).to_broadcast([128, 8, 16]), ALU.mult)
        dah_b = sm[:, 1536:1664].bitcast(BF16)[:, 0:128]
        dal_b = sm[:, 1536:1664].bitcast(BF16)[:, 128:256]
        S.copy(dah_b, da)
        S.tt(dahi, da, dah_b, ALU.subtract)
        S.copy(dal_b, dahi)
        pcum = self.bank(1)
        S.mm(pcum[:, 0:128], [(self.tri_bf[:], dah_b), (self.tri_bf[:], dal_b)])
        S.mm(pcum[:, 128:256], [(self.ones_bf[:], dah_b), (self.ones_bf[:], dal_b)])
        S.copy(cum, pcum[:, 0:128], eng="act")
        S.copy(tot, pcum[:, 128:256], eng="act")
        S.act(ecum, cum, AF.Exp)
        S.tt(decs, tot, cum, ALU.subtract)
        S.act(decs, decs, AF.Exp)
        S.act(edec, tot, AF.Exp)
        for g in range(4):
            w = self.wmain(l, COL_CZ // G + g)
            for tc0 in range(0, 8, 2):
                ps = self.bank(1)
                self.proj_tm(w, G, tc0, ps)
                S.act(szt[:, tc0:tc0 + 2, g * G:(g + 1) * G], ps.rearrange("p (i c) -> p i c", i=2), AF.Silu)
        for c in range(12):
            j = c % 2
            if j == 0:
                wx = self.wmain(l, COL_CX // G + c // 2)
            halo_ps = self.psum[:, 6 * 512:6 * 512 + 4]
            ps = self.proj_fm(wx, j, halo_ps)
            S.copy(xb[:, 3:T + 3], ps, eng="act")
            S.copy(xb[:, 0:3], halo_ps[:, 0:3], eng="act")
            S.ts(xc, xb[:, 0:T], self.P(l, O_SCW + 0 * 12 + c), self.P(l, O_SCB + c), ALU.mult, ALU.add)
            for k in range(1, 4):
                S.stt(xc, xb[:, k:k + T], self.P(l, O_SCW + k * 12 + c), xc, ALU.mult, ALU.add)
            if c < 8:
                dst = xsb
            elif c < 10:
                dst = BT[:, c - 8, :]
            else:
                dst = CT[:, c - 10, :]
            S.act(dst, xc, AF.Silu)
            if c < 10:
                pst = self.bank(1).bitcast(BF16)
                for tc in range(8):
                    S.transpose(pst[:, tc * 128:(tc + 1) * 128], dst[:, tc * 128:(tc + 1) * 128], self.ident_bf[:])
                pv = pst.rearrange("p (c v) -> p c v", c=8)
                if c < 8:
                    S.copy(xs_tok[:, :, c * 128:(c + 1) * 128], pv, eng="act")
                else:
                    S.copy(Btok[:, :, c - 8, :], pv, eng="act")
        stT = self.af(44, 4)
        stb = self.ab(48, 2)
        eg = self.ab(50, 4)
        dab = self.ab(54, 2)
        xdt = self.ab(56, 2)
        y = self.af(58, 4)
        cbT = sm[:, 1664:1792].bitcast(BF16)
        h3 = lambda ap: ap.rearrange("p (h q) -> p h q", h=16)

        def state_step(tc, first):
            S.tt(h3(xdt), h3(xdt), decs[:, tc * 16:(tc + 1) * 16].unsqueeze(2).to_broadcast([128, 16, 64]), ALU.mult)
            psn = self.bank(2)
            for g in range(2):
                S.mm(psn[:, g * 512:(g + 1) * 512], [(Btok[:, tc, g, :], xdt[:, g * 512:(g + 1) * 512])])
            if not first:
                S.tt(h3(stT), h3(stT), edec[:, tc * 16:(tc + 1) * 16].unsqueeze(2).to_broadcast([128, 16, 64]), ALU.mult)
                S.tt(stT, stT, psn, ALU.add)
            else:
                S.copy(stT, psn)

        def make_xdt(tc):
            S.tt(h3(xdt), h3(xs_tok[:, tc, :]), dt[:, tc * 16:(tc + 1) * 16].unsqueeze(2).to_broadcast([128, 16, 64]), ALU.mult)

        for tc in range(8):
            make_xdt(tc)
            state_step(tc, tc == 0)
        dtotC = sm[:, 424:440]
        tsum = sm[:, 464:480]
        S.tt(tsum, tot[:, 0:16], tot[:, 16:32], ALU.add)
        for tc in range(2, 8):
            S.tt(tsum, tsum, tot[:, tc * 16:(tc + 1) * 16], ALU.add)
        S.act(dtotC, tsum, AF.Exp)
        bout = self.exchange([(stT, 0, 1024), (dtotC, 1024, 16)], 1040, "C")
        gth = self.af(49, 13)[:, 0:3 * 1040].rearrange("p (r c) -> p r c", r=3)
        S.dma("sp", "cc_ld_C", gth, bout.rearrange("(r p) c -> p r c", p=128)[:, 0:3, :])
        S.memset(stT, 0.0)
        dpr = sm[:, 480:496]
        for m in range(3):
            pm = self.sel[:, m:m + 1]
            S.ts(dpr, gth[:, m, 1024:1040], -1.0, pm, ALU.add, ALU.mult)
            S.ts(dpr, dpr, 1.0, None, ALU.add)
            S.tt(h3(stT), h3(stT), dpr.unsqueeze(2).to_broadcast([128, 16, 64]), ALU.mult)
            S.stt(stT, gth[:, m, 0:1024], pm, stT, ALU.mult, ALU.add)
        drep = rbl[:, 32:48]
        for tc in range(8):
            tsl = slice(tc * 128, (tc + 1) * 128)
            S.copy(stb, stT, eng="act")
            dav = lambda t: t[:, tc * 16:(tc + 1) * 16]
            dab3 = dab.rearrange("p (a h l) -> p a h l", a=2, h=4)
            seg = self.psum[:, 2048:4096]
            for q in range(4):
                hs = slice(q * 4, (q + 1) * 4)
                S.copy(dab3[:, 0, :, :], dav(dah_b)[:, hs].unsqueeze(2).to_broadcast([128, 4, 128]))
                S.copy(dab3[:, 1, :, :], dav(dal_b)[:, hs].unsqueeze(2).to_broadcast([128, 4, 128]))
                bank = seg[:, q * 512:(q + 1) * 512]

                def fn(e, bank=bank):
                    e.matmul(bank, self.ntri_bf[:], dab3[:, 0, :, :], start=True, stop=False)
                    e.matmul(bank, self.ntri_bf[:], dab3[:, 1, :, :], start=False, stop=False)
                    ins = e.matmul(bank, self.ident_bf[:], self.negrep[:, :, :], start=False, stop=False)
                    for hh in range(4):
                        for a in range(2):
                            ins = e.matmul(bank[:, hh * 128:(hh + 1) * 128], dab3[:, a, hh, :], self.tri_bf[:],
                                           start=False, stop=(hh == 3 and a == 1))
                    return ins
                S.add("pe", fn, [bank], [dab, self.ntri_bf[:], self.tri_bf[:], self.negrep[:], self.ident_bf[:]])
                S.act(eg[:, q * 512:(q + 1) * 512], bank, AF.Exp)
            pscb = self.bank(1)[:, 0:256]
            for g in range(2):
                S.mm(pscb[:, g * 128:(g + 1) * 128], [(BT[:, g, tsl], CT[:, g, tsl])])
            S.copy(cbT, pscb, eng="act")
            eg4 = eg.rearrange("p (g h l) -> p g h l", g=2, h=8)
            for g in range(2):
                S.tt(eg4[:, g, :, :], eg4[:, g, :, :], cbT[:, g * 128:(g + 1) * 128].unsqueeze(1).to_broadcast([128, 8, 128]), ALU.mult)
            make_xdt(tc)
            psy = self.bank(2)
            eg3 = eg.rearrange("p (h l) -> p h l", h=16)

            def fny(e, psy=psy):
                ins = None
                for hh in range(16):
                    ins = e.matmul(psy[:, hh * 64:(hh + 1) * 64], eg3[:, hh, :], xdt[:, hh * 64:(hh + 1) * 64],
                                   start=True, stop=True)
                return ins
            S.add("pe", fny, [psy], [eg, xdt])
            psyo = self.bank(2)
            for g in range(2):
                S.mm(psyo[:, g * 512:(g + 1) * 512], [(CT[:, g, tsl], stb[:, g * 512:(g + 1) * 512])])
            S.tt(h3(y), h3(xs_tok[:, tc, :]), drep.unsqueeze(2).to_broadcast([128, 16, 64]), ALU.mult)
            S.tt(y, y, psy, ALU.add)
            yo = self.af(50, 4)
            S.tt(h3(yo), h3(psyo), ecum[:, tc * 16:(tc + 1) * 16].unsqueeze(2).to_broadcast([128, 16, 64]), ALU.mult)
            S.tt(y, y, yo, ALU.add)
            state_step(tc, False)
            S.tt(y, y, szt[:, tc, :], ALU.mult)
            ss = sm[:, 496:498]
            junk = self.af(50, 2)
            for g in range(2):
                S.act(junk[:, 0:512], y[:, g * 512:(g + 1) * 512], AF.Square, accum_out=ss[:, g:g + 1])
            S.act(ss, ss, AF.Sqrt, bias=EPS, scale=1.0 / 512.0)
            S.recip(ss, ss)
            ot = self.ab(52, 2)
            for g in range(2):
                S.ts(ot[:, g * 512:(g + 1) * 512], y[:, g * 512:(g + 1) * 512], ss[:, g:g + 1], None, ALU.mult)
            pst = self.bank(1).bitcast(BF16)
            for cc in range(8):
                S.transpose(pst[:, cc * 128:(cc + 1) * 128], ot[:, cc * 128:(cc + 1) * 128], self.ident_bf[:])
            S.tt(self.oT[:, :, tsl], pst.rearrange("p (c t) -> p c t", c=8),
                 self.P(l, O_SNW, 8).unsqueeze(2).to_broadcast([128, 8, 128]), ALU.mult)


def _to_fm(xs):
    return np.ascontiguousarray(xs.reshape(xs.shape[0], KC, 128).transpose(2, 1, 0))


def host_prep_layer(inp, l, x):
    f32 = np.float32
    one = {k: (np.asarray(v)[l:l + 1] if k not in ("x", "final_norm_w", "hgrn_lb_logits") else np.asarray(v))
           for k, v in inp.items()}
    pp = np.zeros((128, NP), f32)
    lbl = np.asarray(inp["hgrn_lb_logits"]).astype(f32)
    pp[:, PP_LBL:PP_LBL + 32] = lbl.reshape(4, 8, 128).transpose(2, 0, 1).reshape(128, 32)
    pp[:, PP_FNW:PP_FNW + 16] = fm(np.asarray(inp["final_norm_w"]).astype(f32), 16)
    rb = np.zeros((128, NR), f32)
    b = PP_L0
    pp[:, b + O_NW:b + O_NW + 16] = fm(one["norm_w"][0], 16)
    pp[:, b + O_HNW:b + O_HNW + 8] = fm(one["hgrn_norm_w"][0], 8)
    for k in range(4):
        pp[:, b + O_RCW + k * 8:b + O_RCW + k * 8 + 8] = fm(one["rglru_conv_w"][0, k], 8)
        pp[:, b + O_SCW + k * 12:b + O_SCW + k * 12 + 12] = fm(one["ssd_conv_w"][0, k], 12)
    pp[:, b + O_RCB:b + O_RCB + 8] = fm(one["rglru_conv_b"][0], 8)
    pp[:, b + O_BA:b + O_BA + 8] = fm(one["rglru_ba"][0].reshape(-1), 8)
    pp[:, b + O_BX:b + O_BX + 8] = fm(one["rglru_bx"][0].reshape(-1), 8)
    pp[:, b + O_LAM:b + O_LAM + 8] = fm(one["rglru_lambda"][0], 8)
    pp[:, b + O_SCB:b + O_SCB + 12] = fm(one["ssd_conv_b"][0], 12)
    pp[:, b + O_SNW:b + O_SNW + 8] = fm(one["ssd_norm_w"][0], 8)
    rb[:, 0:16] = one["ssd_dt_bias"][0][None, :]
    rb[:, 16:32] = one["ssd_a_log"][0][None, :]
    rb[:, 32:48] = one["ssd_d"][0][None, :]
    w_in = one["w_in"]
    wmain = np.ascontiguousarray(w_in[:, :, :COL_DT].reshape(1, KC, 128, NG_MAIN, G).transpose(0, 3, 2, 1, 4))
    wdt = np.ascontiguousarray(w_in[:, :, COL_DT:COL_G].reshape(1, KC, 128, 16).transpose(0, 2, 1, 3))
    wgate = np.ascontiguousarray(w_in[:, :, COL_G:].reshape(1, KC, 128, NG_GATE, G).transpose(0, 3, 2, 1, 4))
    wout = np.ascontiguousarray(one["w_out"].reshape(1, 3, 8, 128, 8, G).transpose(0, 1, 4, 3, 2, 5))
    wab = np.ascontiguousarray(np.stack([one["rglru_wa"], one["rglru_wx"]], 1).transpose(0, 3, 1, 2, 4))
    mlb = np.zeros((128, 4), f32)
    mlb[:, 1:l + 1] = 1.0
    shared = dict(pp=pp, rb=rb, cst=host_consts(), mlb=mlb,
                  wmain=wmain.reshape(1, NG_MAIN, 128, KC * G), wdt=wdt.reshape(1, 128, KC * 16),
                  wgate=wgate.reshape(1, NG_GATE, 128, KC * G), wout=wout.reshape(1, 3, 8, 128, 8 * G),
                  wab=wab.reshape(1, 128, 2 * 8 * 128))
    percore = []
    for r in range(8):
        bb, s = r // 4, r % 4
        xT = _to_fm(x[bb, s * T:(s + 1) * T, :])
        xh = np.zeros((128, KC, 4), f32)
        if s > 0:
            xh[:, :, 0:3] = x[bb, s * T - 3:s * T, :].reshape(3, KC, 128).transpose(2, 1, 0)
        sel = np.zeros((128, 8), f32)
        for m in range(4):
            sel[:, m] = 1.0 if m < s else 0.0
            sel[:, 4 + m] = 1.0 if m == s - 1 else 0.0
        percore.append(dict(xT=xT, xh=xh, sel=sel))
    return shared, percore


_PROG = {}


def kernel(**inputs):
    inp = {k: np.ascontiguousarray(np.asarray(v), dtype=np.float32) for k, v in inputs.items()}
    if "fused" not in _PROG:
        _PROG["fused"] = Prog(L=4, branches="BAC", dbg=False, generic=False)
    prog = _PROG["fused"]
    shared, percore = host_prep(inp, 4)
    shared["wmain"] = shared["wmain"].reshape(4, NG_MAIN, 128, KC * G)
    shared["wdt"] = shared["wdt"].reshape(4, 128, KC * 16)
    shared["wgate"] = shared["wgate"].reshape(4, NG_GATE, 128, KC * G)
    shared["wout"] = shared["wout"].reshape(4, 3, 8, 128, 8 * G)
    shared["wab"] = shared["wab"].reshape(4, 128, 2 * 8 * 128)
    shared["mlb"] = np.zeros((128, 4), np.float32)
    in_maps = [dict(shared, **pc) for pc in percore]
    res = run_bass_kernel_spmd(prog.nc, in_maps, core_ids=list(range(8)))
    out = np.zeros((2, 4 * T, 2048), np.float32)
    for r in range(8):
        bb, s = r // 4, r % 4
        o = np.asarray(res.results[r]["out"])
        out[bb, s * T:(s + 1) * T, :] = o.transpose(2, 1, 0).reshape(T, 2048)
    return out
```
